# Optimizing a Trainium2 kernel written in Bass

```python
import math
import jax, jax.numpy as jnp
from jax import lax
import numpy as np

D_MODEL = 1024
BATCH = 16
SEQ = 2048
DEPTH = 4

D_CONV = D_MODEL
CONV_K = 3
D_SSM = D_MODEL
SSM_HEAD_DIM = 64
SSM_HEADS = D_SSM // SSM_HEAD_DIM
SSM_GROUPS = 2
HEADS_PER_GROUP = SSM_HEADS // SSM_GROUPS
D_STATE = 128
SSM_CONV_K = 4
CHUNK = 128
SSM_CONV_DIM = D_SSM + 2 * SSM_GROUPS * D_STATE
D_MIX = D_CONV + D_SSM
IN_COLS = 3 * D_CONV + D_SSM + SSM_CONV_DIM + SSM_HEADS
D_FF = 4 * D_MODEL
EPS = 1e-6

kernel_name = "hybrid_shortconv_ssd_parallel_groups"


def _rmsnorm(x, g):
    xf = x.astype(jnp.float32)
    y = xf * lax.rsqrt(jnp.mean(xf * xf, axis=-1, keepdims=True) + EPS)
    return (y * g.astype(jnp.float32)).astype(x.dtype)


def _causal_depthwise_conv(u, w):
    k, ch = w.shape
    return lax.conv_general_dilated(
        u, w[:, None, :].astype(u.dtype), window_strides=(1,), padding=[(k - 1, 0)],
        dimension_numbers=("NWC", "WIO", "NWC"), feature_group_count=ch)


def _ssd_chunked(x, dt, a, b, c):
    bsz, t = x.shape[:2]
    nc = t // CHUNK
    g, e, p, n = SSM_GROUPS, HEADS_PER_GROUP, SSM_HEAD_DIM, D_STATE
    xdt = (x * dt[..., None]).reshape(bsz, nc, CHUNK, g, e, p)
    adt = (dt * a).reshape(bsz, nc, CHUNK, g, e).transpose(0, 1, 3, 4, 2)
    bc = b.reshape(bsz, nc, CHUNK, g, n)
    cc = c.reshape(bsz, nc, CHUNK, g, n)
    cs = jnp.cumsum(adt, axis=-1)
    mask = jnp.tril(jnp.ones((CHUNK, CHUNK), dtype=bool))
    seg = jnp.where(mask, cs[..., :, None] - cs[..., None, :], -jnp.inf)
    decay_ls = jnp.exp(seg)
    scores = jnp.einsum("bclgn,bcsgn->bcgls", cc, bc)
    m = scores[:, :, :, None] * decay_ls
    y_diag = jnp.einsum("bcgels,bcsgep->bclgep", m, xdt)
    decay_to_end = jnp.exp(cs[..., -1:] - cs)
    states = jnp.einsum("bclgn,bcgel,bclgep->bcgepn", bc, decay_to_end, xdt)
    chunk_decay = jnp.exp(cs[..., -1])

    def step(carry, inp):
        s_c, d_c = inp
        return carry * d_c[..., None, None] + s_c, carry

    init = jnp.zeros((bsz, g, e, p, n), jnp.float32)
    _, prev = lax.scan(step, init, (jnp.moveaxis(states, 1, 0), jnp.moveaxis(chunk_decay, 1, 0)))
    prev = jnp.moveaxis(prev, 0, 1)
    y_off = jnp.einsum("bclgn,bcgepn,bcgel->bclgep", cc, prev, jnp.exp(cs))
    return (y_diag + y_off).reshape(bsz, t, SSM_HEADS, p)


def setup_inputs(seed: int = 0) -> dict:
    key = jax.random.key(seed)
    ks = jax.random.split(key, 20)
    f32 = jnp.float32
    nrm = lambda k, s, scale: jax.random.normal(k, s, f32) * scale
    gain = lambda k, s: 1.0 + 0.02 * jax.random.normal(k, s, f32)
    x = jax.random.normal(ks[0], (BATCH, SEQ, D_MODEL), f32)
    dt_min, dt_max = 1e-3, 1e-1
    u = jax.random.uniform(ks[6], (DEPTH, SSM_HEADS), f32)
    dt0 = jnp.exp(u * (math.log(dt_max) - math.log(dt_min)) + math.log(dt_min))
    dt_bias = dt0 + jnp.log(-jnp.expm1(-dt0))
    a_log = jnp.log(jax.random.uniform(ks[7], (DEPTH, SSM_HEADS), f32, 1.0, 16.0))
    return {
        "x": x,
        "norm_mix_pre": gain(ks[1], (DEPTH, D_MODEL)),
        "w_in": nrm(ks[2], (DEPTH, D_MODEL, IN_COLS), D_MODEL ** -0.5),
        "conv_a_w": nrm(ks[3], (DEPTH, CONV_K, D_CONV), CONV_K ** -0.5),
        "ssm_conv_w": nrm(ks[4], (DEPTH, SSM_CONV_K, SSM_CONV_DIM), SSM_CONV_K ** -0.5),
        "ssm_conv_b": nrm(ks[5], (DEPTH, SSM_CONV_DIM), 0.02),
        "dt_bias": dt_bias,
        "a_log": a_log,
        "d_skip": gain(ks[8], (DEPTH, SSM_HEADS)),
        "conv_out_norm": gain(ks[9], (DEPTH, D_CONV)),
        "ssm_out_norm": gain(ks[10], (DEPTH, D_SSM)),
        "w_out": nrm(ks[11], (DEPTH, D_MIX, D_MODEL), D_MIX ** -0.5),
        "norm_mix_post": gain(ks[12], (DEPTH, D_MODEL)),
        "norm_mlp_pre": gain(ks[13], (DEPTH, D_MODEL)),
        "w_up": nrm(ks[14], (DEPTH, D_MODEL, D_FF), D_MODEL ** -0.5),
        "w_down": nrm(ks[15], (DEPTH, D_FF, D_MODEL), D_FF ** -0.5),
        "norm_mlp_post": gain(ks[16], (DEPTH, D_MODEL)),
    }


def reference(x, norm_mix_pre, w_in, conv_a_w, ssm_conv_w, ssm_conv_b, dt_bias, a_log, d_skip,
              conv_out_norm, ssm_out_norm, w_out, norm_mix_post, norm_mlp_pre, w_up, w_down,
              norm_mlp_post):
    bsz, t, _ = x.shape
    split_at = [D_CONV, 2 * D_CONV, 3 * D_CONV, 3 * D_CONV + D_SSM,
                3 * D_CONV + D_SSM + SSM_CONV_DIM]
    for i in range(DEPTH):
        h = _rmsnorm(x, norm_mix_pre[i])
        proj = jnp.einsum("btd,de->bte", h, w_in[i])
        x_a, c_a, b_a, z, xbc, dt_raw = jnp.split(proj, split_at, axis=-1)
        y_a = b_a * _causal_depthwise_conv(c_a * x_a, conv_a_w[i])
        y_a = _rmsnorm(y_a, conv_out_norm[i])
        xbc = _causal_depthwise_conv(xbc, ssm_conv_w[i]) + ssm_conv_b[i].astype(xbc.dtype)
        xbc = jax.nn.silu(xbc)
        xs, bs, cs_ = jnp.split(xbc, [D_SSM, D_SSM + SSM_GROUPS * D_STATE], axis=-1)
        xs = xs.reshape(bsz, t, SSM_HEADS, SSM_HEAD_DIM).astype(jnp.float32)
        bs = bs.reshape(bsz, t, SSM_GROUPS, D_STATE).astype(jnp.float32)
        cs_ = cs_.reshape(bsz, t, SSM_GROUPS, D_STATE).astype(jnp.float32)
        dt = jax.nn.softplus(dt_raw.astype(jnp.float32) + dt_bias[i].astype(jnp.float32))
        a = -jnp.exp(a_log[i].astype(jnp.float32))
        y_s = _ssd_chunked(xs, dt, a, bs, cs_) + d_skip[i].astype(jnp.float32)[:, None] * xs
        y_s = y_s.reshape(bsz, t, D_SSM) * jax.nn.silu(z.astype(jnp.float32))
        y_s = _rmsnorm(y_s.reshape(bsz, t, SSM_GROUPS, D_SSM // SSM_GROUPS),
                       ssm_out_norm[i].reshape(SSM_GROUPS, D_SSM // SSM_GROUPS))
        y_s = y_s.reshape(bsz, t, D_SSM).astype(x.dtype)
        mix = jnp.einsum("bte,ed->btd", jnp.concatenate([y_a, y_s], axis=-1), w_out[i])
        x = x + _rmsnorm(mix, norm_mix_post[i])
        h = _rmsnorm(x, norm_mlp_pre[i])
        f = jnp.square(jax.nn.relu(jnp.einsum("btd,df->btf", h, w_up[i])))
        f = jnp.einsum("btf,fd->btd", f, w_down[i])
        x = x + _rmsnorm(f, norm_mlp_post[i])
    return x
```

```python
import numpy as np
import concourse.bass as bass
import concourse.mybir as mybir
from concourse.bass_utils import run_bass_kernel_spmd

F32 = mybir.dt.float32
BF16 = mybir.dt.bfloat16
ALU = mybir.AluOpType
AF = mybir.ActivationFunctionType


class Tile:
    __slots__ = ("h", "name", "w", "r", "psum")

    def __init__(self, h, name, psum=False):
        self.h = h
        self.name = name
        self.w = None
        self.r = {}
        self.psum = psum


class Prog:
    NDMA = 24

    def __init__(self, nc):
        self.nc = nc
        self.eng = {"pe": nc.tensor, "dve": nc.vector, "act": nc.scalar, "pool": nc.gpsimd, "sp": nc.sync}
        self.sem = {}
        self.cnt = {}
        for e in ("pe", "dve", "act", "pool"):
            self.sem[e] = nc.alloc_semaphore("s_" + e)
            self.cnt[e] = 0
        for i in range(self.NDMA):
            k = "d%d" % i
            self.sem[k] = nc.alloc_semaphore("s_" + k)
            self.cnt[k] = 0
        self.seen = {e: {} for e in self.eng}
        self.dma_rr = 0
        self.n_inst = 0
        self.n_wait = 0
        self.dry = False
        self.mark_half = False

    def sb(self, name, shape, dtype):
        return Tile(self.nc.alloc_sbuf_tensor(name, list(shape), dtype), name)

    def ps(self, name, shape, dtype):
        return Tile(self.nc.alloc_psum_tensor(name, list(shape), dtype), name, psum=True)

    def need(self, eng, ev):
        key, val = ev
        if self.seen[eng].get(key, 0) >= val:
            return
        assert val <= self.cnt[key], "wait on pending event %s %d > %d" % (key, val, self.cnt[key])
        self.eng[eng].wait_ge(self.sem[key], val)
        self.seen[eng][key] = val
        self.n_wait += 1

    def _deps(self, eng, rd, wr):
        for t in rd:
            if t.w is not None:
                if not (t.w[0] == eng and eng == "pe"):
                    self.need(eng, t.w)
            if t.psum:
                for k, v in t.r.items():
                    if k != eng:
                        self.need(eng, (k, v))
        for t in wr:
            if t.w is not None and not (t.w[0] == eng and eng == "pe"):
                self.need(eng, t.w)
            for k, v in t.r.items():
                if not (k == eng and eng == "pe"):
                    self.need(eng, (k, v))

    def op(self, eng, fn, rd=(), wr=(), inc=True):
        if self.dry:
            return None
        self._deps(eng, rd, wr)
        inst = fn(self.eng[eng])
        val = self.cnt[eng] + 1
        if inc:
            inst.then_inc(self.sem[eng], 1)
            self.cnt[eng] = val
        else:
            assert eng == "pe"
        for t in rd:
            if t.r.get(eng, 0) < val:
                t.r[eng] = val
        for t in wr:
            t.w = (eng, val)
            t.r = {}
        self.n_inst += 1
        return inst

    def dma(self, out_ap, in_ap, rd=(), wr=(), queue="sp"):
        if self.dry:
            return None
        k = "d%d" % self.dma_rr
        self.dma_rr = (self.dma_rr + 1) % self.NDMA
        if self.cnt[k]:
            self.need(queue, (k, self.cnt[k]))
        self._deps(queue, rd, wr)
        inst = self.eng[queue].dma_start(out=out_ap, in_=in_ap)
        inst.then_inc(self.sem[k], 16)
        self.cnt[k] += 16
        val = self.cnt[k]
        for t in rd:
            t.r[k] = val
        for t in wr:
            t.w = (k, val)
            t.r = {}
        self.n_inst += 1
        return inst

    def handoff(self, src, dst):
        if self.dry:
            return
        ev = {}
        for t in src:
            if t.w is not None:
                ev[t.w[0]] = max(ev.get(t.w[0], 0), t.w[1])
            for k, v in t.r.items():
                ev[k] = max(ev.get(k, 0), v)
        for d in dst:
            for k, v in ev.items():
                if d.r.get(k, 0) < v:
                    d.r[k] = v

    def barrier(self, engines=None):
        for e in (engines or self.eng):
            for k, v in self.cnt.items():
                if v and k != e:
                    self.need(e, (k, v))

    def finish(self):
        self.barrier(["sp"])


D = 1024
KD = 8
NHEAD = 16
HP = 64
DFF = 4096
EPS = 1e-6
NCH = 124
CH_Z = 24
CH_XBC = 32
CH_OUT = 44
CH_UP = 60
CH_DOWN = 92
NPV = 132
PV_MIXPRE, PV_CVA, PV_CVS, PV_CVB, PV_MIXPOST, PV_MLPPRE, PV_MLPPOST, PV_GA, PV_GS = 0, 8, 32, 80, 92, 100, 108, 116, 124
MASKNEG = -30000.0
MASK_BF = True
SSD_ACC = True


def build_program(NL=4, NB=2, SEQ=2048, TT=256, NWB=7):
    assert NB == 2, "two interleaved streams = the two sequences owned by this core"
    NTOK = NB * SEQ
    NT = SEQ // TT
    NC_ = TT // 128
    nc = bass.Bass("TRN2", target_bir_lowering=False)
    xT_d = nc.dram_tensor("xT", [D, NTOK], F32, kind="ExternalInput").ap()
    wch_d = nc.dram_tensor("wch", [NL, NCH, 128, 1024], F32, kind="ExternalInput").ap()
    wdt_d = nc.dram_tensor("wdt", [NL, 128, 128], F32, kind="ExternalInput").ap()
    pv_d = nc.dram_tensor("pv", [128, NL * NPV], F32, kind="ExternalInput").ap()
    pb_d = nc.dram_tensor("pb", [128, NL * 48], F32, kind="ExternalInput").ap()
    yT_d = nc.dram_tensor("yT", [D, NTOK], F32, kind="ExternalOutput").ap()
    wsc_d = nc.dram_tensor("wsc", [NL, NCH, 128, 1024], BF16, kind="Internal").ap()

    P = Prog(nc)
    sb, ps, op, dma = P.sb, P.ps, P.op, P.dma

    PV = sb("PV", [128, NL * NPV], F32)
    PBt = sb("PBt", [128, NL * 48], F32)
    AB = sb("AB", [128, NL * 16], F32)
    WDT = sb("WDT", [128, NL, 128], BF16)
    I32 = sb("I32", [128, 128], F32)
    IBF = sb("IBF", [128, 128], BF16)
    UT = sb("UT", [128, 128], F32)
    UTB = sb("UTB", [128, 128], BF16)
    ONE32 = sb("ONE32", [128, 128], F32)
    MASKB = sb("MASKB", [128, 128], BF16)
    MASK32 = sb("MASK32", [128, 128], F32)
    ONESD = sb("ONESD", [128, 128], BF16)
    ONESG = sb("ONESG", [128, 128], BF16)
    CST = sb("CST", [128, 2], F32)

    dma(PV.h[:, :], pv_d, wr=[PV])
    dma(PBt.h[:, :], pb_d, wr=[PBt])
    op("pool", lambda e: e.memset(I32.h[:, :], 1.0), wr=[I32])
    op("pool", lambda e: e.affine_select(out=I32.h[:, :], in_=I32.h[:, :], pattern=[[1, 128]], compare_op=ALU.is_equal,
                                         fill=0.0, base=0, channel_multiplier=-1), rd=[I32], wr=[I32])
    op("pool", lambda e: e.memset(UT.h[:, :], 1.0), wr=[UT])
    op("pool", lambda e: e.affine_select(out=UT.h[:, :], in_=UT.h[:, :], pattern=[[1, 128]], compare_op=ALU.is_ge,
                                         fill=0.0, base=0, channel_multiplier=-1), rd=[UT], wr=[UT])
    op("pool", lambda e: e.memset(MASK32.h[:, :], 0.0), wr=[MASK32])
    op("pool", lambda e: e.affine_select(out=MASK32.h[:, :], in_=MASK32.h[:, :], pattern=[[1, 128]], compare_op=ALU.is_ge,
                                         fill=MASKNEG, base=0, channel_multiplier=-1), rd=[MASK32], wr=[MASK32])
    op("dve", lambda e: e.tensor_copy(out=MASKB.h[:, :], in_=MASK32.h[:, :]), rd=[MASK32], wr=[MASKB])
    op("pool", lambda e: e.memset(ONE32.h[:, :], 1.0), wr=[ONE32])
    op("pool", lambda e: e.memset(ONESD.h[:, :], 1.0 / D), wr=[ONESD])
    op("pool", lambda e: e.memset(ONESG.h[:, :], 1.0 / 512), wr=[ONESG])
    op("pool", lambda e: e.memset(CST.h[:, 0:1], 1.0), wr=[CST])
    op("pool", lambda e: e.memset(CST.h[:, 1:2], EPS), wr=[CST])
    op("dve", lambda e: e.tensor_copy(out=IBF.h[:, :], in_=I32.h[:, :]), rd=[I32], wr=[IBF])
    op("dve", lambda e: e.tensor_copy(out=UTB.h[:, :], in_=UT.h[:, :]), rd=[UT], wr=[UTB])
    for l in range(NL):
        op("act", lambda e, l=l: e.activation(out=AB.h[:, l * 16:(l + 1) * 16], in_=PBt.h[:, l * 48 + 16:l * 48 + 32], func=AF.Exp),
           rd=[PBt], wr=[AB])
    op("dve", lambda e: e.tensor_scalar(out=AB.h[:, :], in0=AB.h[:, :], scalar1=-1.0, scalar2=None, op0=ALU.mult), rd=[AB], wr=[AB])

    NSTG = 6
    stg_h = [nc.sbuf_tensor("STG%d" % i, [128, 2, 1024], F32) for i in range(NSTG)]
    stb_h = [nc.sbuf_tensor("STB%d" % i, [128, 2, 1024], BF16) for i in range(NSTG)]
    wds_h = nc.sbuf_tensor("WDS", [128, 128], F32)
    guards = stg_h + stb_h + [wds_h]
    handles = [g.__enter__() for g in guards]
    STG = [Tile(h, "STG") for h in handles[:NSTG]]
    STB = [Tile(h, "STB") for h in handles[NSTG:2 * NSTG]]
    WDS = Tile(handles[-1], "WDS")
    cast_rr = 0
    for l in range(NL):
        dma(WDS.h[:, :], wdt_d[l], wr=[WDS])
        op("dve", lambda e, l=l: e.tensor_copy(out=WDT.h[:, l, :], in_=WDS.h[:, :]), rd=[WDS], wr=[WDT])
    jobs = [(l, c0) for l in range(NL) for c0 in range(0, NCH, 2)]
    LOOKAHEAD = NSTG - 2

    def load(i):
        l, c0 = jobs[i]
        stg = STG[i % NSTG]
        dma(stg.h[:, :, :], wch_d[l, c0:c0 + 2].rearrange("c p f -> p c f"), wr=[stg])

    for i in range(min(LOOKAHEAD, len(jobs))):
        load(i)
    for i, (l, c0) in enumerate(jobs):
        stg, stb = STG[i % NSTG], STB[i % NSTG]
        if CH_OUT <= c0 < CH_UP:
            for cc in range(2):
                half = (c0 + cc - CH_OUT) % 2
                base = l * NPV + (PV_GA if half == 0 else PV_GS)
                for k in range(KD):
                    if k % 2 == 0:
                        op("dve", lambda e, cc=cc, k=k, base=base, stg=stg, stb=stb: e.tensor_scalar(
                            out=stb.h[:, cc, k * 128:(k + 1) * 128], in0=stg.h[:, cc, k * 128:(k + 1) * 128],
                            scalar1=PV.h[:, base + k:base + k + 1], scalar2=None, op0=ALU.mult), rd=[stg, PV], wr=[stb])
                    else:
                        op("act", lambda e, cc=cc, k=k, base=base, stg=stg, stb=stb: e.activation(
                            out=stb.h[:, cc, k * 128:(k + 1) * 128], in_=stg.h[:, cc, k * 128:(k + 1) * 128], func=AF.Copy,
                            scale=PV.h[:, base + k:base + k + 1]), rd=[stg, PV], wr=[stb])
        else:
            eng = ("dve", "act")[cast_rr % 2]
            cast_rr += 1
            if eng == "act":
                op("act", lambda e, stg=stg, stb=stb: e.activation(out=stb.h[:, :, :], in_=stg.h[:, :, :], func=AF.Copy), rd=[stg], wr=[stb])
            else:
                op(eng, lambda e, stg=stg, stb=stb: e.tensor_copy(out=stb.h[:, :, :], in_=stg.h[:, :, :]), rd=[stg], wr=[stb])
        if i + LOOKAHEAD < len(jobs):
            load(i + LOOKAHEAD)
        dma(wsc_d[l, c0:c0 + 2].rearrange("c p f -> p c f"), stb.h[:, :, :], rd=[stb])
    P.barrier()
    for g in reversed(guards):
        g.__exit__(None, None, None)

    W = 2 * TT
    X = sb("X", [128, KD, W], F32)
    H = sb("H", [128, KD, W], BF16)
    BIG = sb("BIG", [128, 32, W], BF16)
    FT = BIG
    YA = Tile(BIG.h[:, 0:8, :], "YA")
    YS = Tile(BIG.h[:, 8:16, :], "YS")
    XS = Tile(BIG.h[:, 16:24, :], "XS")
    SZ = Tile(BIG.h[:, 24:32, :], "SZ")
    BT = sb("BT", [128, 2, W], BF16)
    CT = sb("CT", [128, 2, W], BF16)
    MIX = sb("MIX", [128, KD, W], F32)
    SQ8 = sb("SQ8", [128, KD, W], BF16)
    RS = sb("RS", [128, W], F32)
    RSA = sb("RSA", [128, W], F32)
    RSS = sb("RSS", [128, 2, W], F32)
    S = [[sb("S%d_%d" % (q, l), [128, 1024], F32) for l in range(NL)] for q in range(2)]
    HA = [sb("HA%d" % l, [128, KD, 2, 2], F32) for l in range(NL)]
    HS = [sb("HS%d" % l, [128, 12, 2, 3], F32) for l in range(NL)]

    UU = [sb("UU%d" % i, [128, 2, TT + 3], F32) for i in range(4)]
    SCR = [sb("SCR%d" % i, [128, W], F32) for i in range(8)]
    UA = US = UU
    VV = XAt = RR = TMP = SCR
    XDTs = [[sb("XDT%d_%d" % (q, i), [128, 512], BF16) for i in range(2)] for q in range(2)]
    XDTEs = [[sb("XDTE%d_%d" % (q, i), [128, 512], BF16) for i in range(2)] for q in range(2)]
    XSDs = [[sb("XSD%d_%d" % (q, i), [128, 512], BF16) for i in range(2)] for q in range(2)]
    DMs = [[sb("DM%d_%d" % (q, i), [128, 8, 128], BF16) for i in range(2)] for q in range(2)]
    Y1s = [[sb("Y1%d_%d" % (q, i), [128, 512], BF16) for i in range(2)] for q in range(2)]
    BTOKs = [sb("BTOK%d" % q, [128, 2, 128], BF16) for q in range(2)]
    SQCs = [[sb("SQC%d_%d" % (q, i), [128, 4, 128], BF16) for i in range(2)] for q in range(2)]
    SBFs = [sb("SBF%d" % q, [128, 1024], BF16) for q in range(2)]
    T0 = sb("T0", [128, 64], F32)
    Mx = sb("Mx", [128, 64], F32)
    NA = sb("NA", [128, 64], F32)
    DT = sb("DT", [128, 64], F32)
    ADT = sb("ADT", [128, 64], F32)
    NCS = sb("NCS", [128, 64], F32)
    ECS = sb("ECS", [128, 64], F32)
    DTE = sb("DTE", [128, 64], F32)
    CD = sb("CD", [128, 64], F32)
    DTDTE = sb("DTDTE", [128, 64], F32)
    WB = [sb("WB%d" % i, [128, KD, 128], BF16) for i in range(NWB)]

    BANKS = [ps("BK%d" % i, [128, 512], F32) for i in range(8)]
    Xk = [Tile(None, "Xk%d" % k) for k in range(KD)]
    Hk = [Tile(None, "Hk%d" % k) for k in range(KD)]
    MIXk = [Tile(None, "MIXk%d" % k) for k in range(KD)]
    SQk = [Tile(None, "SQk%d" % k) for k in range(KD)]
    Bk = [Tile(None, "Bk%d" % k) for k in range(32)]
    YAk, YSk, XSk, SZk, FTk = Bk[0:8], Bk[8:16], Bk[16:24], Bk[24:32], Bk
    state = {"proj": 0, "wnext": 0, "wissued": 0, "rr": 0}
    worder = []

    def issue_weights(upto):
        while state["wissued"] < min(upto, len(worder)):
            g = state["wissued"]
            l, c = worder[g]
            wb = WB[g % NWB]
            dma(wb.h[:, :, :], wsc_d[l, c].rearrange("p (k e) -> p k e", k=KD), wr=[wb])
            state["wissued"] += 1

    def next_w(l, c):
        if P.dry:
            worder.append((l, c))
            return WB[0]
        g = state["wnext"]
        assert worder[g] == (l, c), (worder[g], (l, c))
        issue_weights(g + NWB - 1)
        state["wnext"] += 1
        issue_weights(g + 1)
        return WB[g % NWB]

    def next_proj():
        p = BANKS[state["proj"] % 8]
        state["proj"] += 1
        return p

    def rot(lst):
        state["rr"] += 1
        return lst[state["rr"] % len(lst)]

    def s3(ap):
        return ap.rearrange("p (s t) -> p s t", s=2)

    def proj_fm(pt, wb, src, srck, k0=0, nk=KD, start=True, stop=True):
        for k in range(nk):
            last = (k == nk - 1)
            op("pe", lambda e, k=k: e.matmul(pt.h[:, 0:W], lhsT=wb.h[:, k, :], rhs=src.h[:, k0 + k, :],
                                             start=(start and k == 0), stop=(stop and last)),
               rd=[wb, srck[k0 + k]], wr=[pt], inc=last)

    def rstd_from(ps_ap, ps_tile, out_ap, out_tile):
        op("act", lambda e: e.activation(out=out_ap, in_=ps_ap, func=AF.Ln, bias=CST.h[:, 1:2], scale=1.0), rd=[ps_tile, CST], wr=[out_tile])
        op("act", lambda e: e.activation(out=out_ap, in_=out_ap, func=AF.Exp, scale=-0.5), rd=[out_tile], wr=[out_tile])

    def stats_mm(pt):
        for k in range(KD):
            op("pe", lambda e, k=k: e.matmul(pt.h[:, 0:W], lhsT=ONESD.h[:, :], rhs=SQ8.h[:, k, :], start=(k == 0), stop=(k == KD - 1)),
               rd=[ONESD, SQk[k]], wr=[pt], inc=(k == KD - 1))

    def norm_to_H(l, pvcol, have_sq=False):
        pt = next_proj()
        if not have_sq:
            op("act", lambda e: e.activation(out=SQ8.h[:, :, :], in_=X.h[:, :, :], func=AF.Square), rd=Xk, wr=SQk)
        stats_mm(pt)
        rstd_from(pt.h[:, 0:W], pt, RS.h[:, :], RS)
        for k in range(KD):
            c = l * NPV + pvcol + k
            op("dve", lambda e, k=k, c=c: e.scalar_tensor_tensor(out=H.h[:, k, :], in0=X.h[:, k, :], scalar=PV.h[:, c:c + 1], in1=RS.h[:, :],
                                                                 op0=ALU.mult, op1=ALU.mult), rd=[Xk[k], PV, RS], wr=[Hk[k]])

    def post_norm_residual(l, pvcol, final=False):
        pt = next_proj()
        stats_mm(pt)
        rstd_from(pt.h[:, 0:W], pt, RS.h[:, :], RS)
        tmps = [rot(TMP) for k in range(KD)]
        for k in range(KD):
            c = l * NPV + pvcol + k
            tmp = tmps[k]
            op("dve", lambda e, k=k, c=c, tmp=tmp: e.scalar_tensor_tensor(out=tmp.h[:, :], in0=MIX.h[:, k, :], scalar=PV.h[:, c:c + 1], in1=RS.h[:, :],
                                                                          op0=ALU.mult, op1=ALU.mult), rd=[MIXk[k], PV, RS], wr=[tmp])
        for k in range(KD):
            tmp = tmps[k]
            eng = "dve"
            if final:
                op(eng, lambda e, k=k, tmp=tmp: e.tensor_tensor(out=MIX.h[:, k, :], in0=X.h[:, k, :], in1=tmp.h[:, :], op=ALU.add), rd=[Xk[k], tmp], wr=[MIXk[k]])
                continue
            op(eng, lambda e, k=k, tmp=tmp: e.tensor_tensor(out=X.h[:, k, :], in0=X.h[:, k, :], in1=tmp.h[:, :], op=ALU.add), rd=[Xk[k], tmp], wr=[Xk[k]])
            op("act", lambda e, k=k: e.activation(out=SQ8.h[:, k, :], in_=X.h[:, k, :], func=AF.Square), rd=[Xk[k]], wr=[SQk[k]])

    def bc16(t, h0, nh):
        return t.h[:, h0:h0 + nh].unsqueeze(2).to_broadcast([128, nh, HP])

    def v3(ap):
        return ap.rearrange("p (h d) -> p h d", d=HP)

    def interleave(gens):
        vt = [0.0] * len(gens)
        alive = [True] * len(gens)
        while any(alive):
            i = min((k for k in range(len(gens)) if alive[k]), key=lambda k: vt[k])
            try:
                vt[i] += next(gens[i])
            except StopIteration:
                alive[i] = False

    def group_a_blocks(l, first_tile):
        pvb = l * NPV
        for j in range(KD):
            w1 = next_w(l, 3 * j)
            p1 = next_proj()
            proj_fm(p1, w1, H, Hk)
            w2 = next_w(l, 3 * j + 1)
            p2 = next_proj()
            proj_fm(p2, w2, H, Hk)
            w3 = next_w(l, 3 * j + 2)
            p3 = next_proj()
            proj_fm(p3, w3, H, Hk)
            xa, u, v = rot(XAt), rot(UA), rot(VV)
            op("act", lambda e: e.activation(out=xa.h[:, :], in_=p1.h[:, 0:W], func=AF.Copy), rd=[p1], wr=[xa])
            if first_tile:
                op("pool", lambda e: e.memset(u.h[:, :, 1:3], 0.0), wr=[u])
            else:
                op("pool", lambda e: e.tensor_copy(out=u.h[:, :, 1:3], in_=HA[l].h[:, j, :, :]), rd=[HA[l]], wr=[u])
            op("dve", lambda e: e.tensor_tensor(out=u.h[:, :, 3:3 + TT], in0=s3(p2.h[:, 0:W]), in1=s3(xa.h[:, :]), op=ALU.mult), rd=[p2, xa], wr=[u])
            op("pool", lambda e: e.tensor_copy(out=HA[l].h[:, j, :, :], in_=u.h[:, :, TT + 1:TT + 3]), rd=[u], wr=[HA[l]])
            c0 = pvb + PV_CVA + j
            op("act", lambda e: e.activation(out=s3(v.h[:, :]), in_=u.h[:, :, 3:3 + TT], func=AF.Copy, scale=PV.h[:, c0 + 16:c0 + 17]), rd=[u, PV], wr=[v])
            op("dve", lambda e: e.scalar_tensor_tensor(out=s3(v.h[:, :]), in0=u.h[:, :, 2:2 + TT], scalar=PV.h[:, c0 + 8:c0 + 9], in1=s3(v.h[:, :]),
                                                       op0=ALU.mult, op1=ALU.add), rd=[u, PV, v], wr=[v])
            op("dve", lambda e: e.scalar_tensor_tensor(out=s3(v.h[:, :]), in0=u.h[:, :, 1:1 + TT], scalar=PV.h[:, c0:c0 + 1], in1=s3(v.h[:, :]),
                                                       op0=ALU.mult, op1=ALU.add), rd=[u, PV, v], wr=[v])
            op("dve", lambda e: e.tensor_tensor(out=YA.h[:, j, :], in0=p3.h[:, 0:W], in1=v.h[:, :], op=ALU.mult), rd=[p3, v], wr=[YAk[j]])
            op("act", lambda e: e.activation(out=SQ8.h[:, j, :], in_=YA.h[:, j, :], func=AF.Square), rd=[YAk[j]], wr=[SQk[j]])
            yield 3.0

    def mixer(l, first_tile, have_sq):
        pvb = l * NPV
        norm_to_H(l, PV_MIXPRE, have_sq=have_sq)
        def z_block(j):
            w = next_w(l, CH_Z + j)
            p = next_proj()
            proj_fm(p, w, H, Hk)
            op("act", lambda e: e.activation(out=SZ.h[:, j, :], in_=p.h[:, 0:W], func=AF.Silu), rd=[p], wr=[SZk[j]])
        zj = [0]
        for j0 in range(0, 12, 2):
            js = (j0, j0 + 1)
            ps_, us_, vs_ = [], [], []
            for j in js:
                w = next_w(l, CH_XBC + j)
                p = next_proj()
                proj_fm(p, w, H, Hk)
                ps_.append(p)
                us_.append(rot(US))
                vs_.append(rot(VV))
            for i, j in enumerate(js):
                u = us_[i]
                if first_tile:
                    op("pool", lambda e: e.memset(u.h[:, :, 0:3], 0.0), wr=[u])
                else:
                    op("pool", lambda e: e.tensor_copy(out=u.h[:, :, 0:3], in_=HS[l].h[:, j, :, :]), rd=[HS[l]], wr=[u])
                op("act", lambda e: e.activation(out=u.h[:, :, 3:3 + TT], in_=s3(ps_[i].h[:, 0:W]), func=AF.Copy), rd=[ps_[i]], wr=[u])
                op("pool", lambda e: e.tensor_copy(out=HS[l].h[:, j, :, :], in_=u.h[:, :, TT:TT + 3]), rd=[u], wr=[HS[l]])
            for i, j in enumerate(js):
                u, v = us_[i], vs_[i]
                c0 = pvb + PV_CVS + j
                cb = pvb + PV_CVB + j
                op("dve", lambda e: e.tensor_scalar(out=s3(v.h[:, :]), in0=u.h[:, :, 3:3 + TT], scalar1=PV.h[:, c0 + 36:c0 + 37], scalar2=PV.h[:, cb:cb + 1],
                                                    op0=ALU.mult, op1=ALU.add), rd=[u, PV], wr=[v])
            for kk in (2, 1, 0):
                for i, j in enumerate(js):
                    u, v = us_[i], vs_[i]
                    c0 = pvb + PV_CVS + j
                    op("dve", lambda e: e.scalar_tensor_tensor(out=s3(v.h[:, :]), in0=u.h[:, :, kk:kk + TT], scalar=PV.h[:, c0 + 12 * kk:c0 + 12 * kk + 1],
                                                               in1=s3(v.h[:, :]), op0=ALU.mult, op1=ALU.add), rd=[u, PV, v], wr=[v])
            for i, j in enumerate(js):
                v = vs_[i]
                if j < 8:
                    dst, dap = XSk[j], XS.h[:, j, :]
                elif j < 10:
                    dst, dap = BT, BT.h[:, j - 8, :]
                else:
                    dst, dap = CT, CT.h[:, j - 10, :]
                op("act", lambda e: e.activation(out=dap, in_=v.h[:, :], func=AF.Silu), rd=[v], wr=[dst])
            if zj[0] < KD:
                z_block(zj[0])
                zj[0] += 1
            if j0 % 4 == 0 or j0 >= 8:
                if zj[0] < KD:
                    z_block(zj[0])
                    zj[0] += 1
        while zj[0] < KD:
            z_block(zj[0])
            zj[0] += 1
        ssd_prep(l)
        deferred = []

        def seq_chunks(q):
            for c in range(NC_):
                yield from ssd_chunk(l, q, c, first_tile and c == 0, defer=(deferred if c == NC_ - 1 else None))
        def filler():
            yield from group_a_blocks(l, first_tile)
            pt = next_proj()
            stats_mm(pt)
            op("act", lambda e: e.activation(out=RSA.h[:, :], in_=pt.h[:, 0:W], func=AF.Ln, bias=CST.h[:, 1:2], scale=1.0), rd=[pt, CST], wr=[RSA])
            op("act", lambda e: e.activation(out=RSA.h[:, :], in_=RSA.h[:, :], func=AF.Exp, scale=-0.5), rd=[RSA], wr=[RSA])
            for j in range(KD):
                op("dve", lambda e, j=j: e.tensor_tensor(out=YA.h[:, j, :], in0=YA.h[:, j, :], in1=RSA.h[:, :], op=ALU.mult), rd=[YAk[j], RSA], wr=[YAk[j]])
            yield 2.0
            for db in range(KD):
                p = next_proj()
                w = next_w(l, CH_OUT + 2 * db)
                proj_fm(p, w, YA, YAk)
                op("act", lambda e: e.activation(out=MIX.h[:, db, :], in_=p.h[:, 0:W], func=AF.Copy), rd=[p], wr=[MIXk[db]])
                yield 1.5
        interleave([seq_chunks(0), seq_chunks(1), filler()])
        op("act", lambda e: e.activation(out=RSS.h[:, :, :], in_=RSS.h[:, :, :], func=AF.Ln, bias=CST.h[:, 1:2], scale=1.0), rd=[RSS, CST], wr=[RSS])
        op("act", lambda e: e.activation(out=RSS.h[:, :, :], in_=RSS.h[:, :, :], func=AF.Exp, scale=-0.5), rd=[RSS], wr=[RSS])
        for j in range(KD):
            g = j // 4
            eng = "dve"
            op(eng, lambda e, j=j, g=g: e.tensor_tensor(out=YS.h[:, j, :], in0=YS.h[:, j, :], in1=RSS.h[:, g, :], op=ALU.mult), rd=[YSk[j], RSS], wr=[YSk[j]])
        for db in range(KD):
            p = next_proj()
            w = next_w(l, CH_OUT + 2 * db + 1)
            proj_fm(p, w, YS, YSk)
            op("dve", lambda e: e.tensor_tensor(out=MIX.h[:, db, :], in0=p.h[:, 0:W], in1=MIX.h[:, db, :], op=ALU.add), rd=[p, MIXk[db]], wr=[MIXk[db]])
            op("act", lambda e: e.activation(out=SQ8.h[:, db, :], in_=MIX.h[:, db, :], func=AF.Square), rd=[MIXk[db]], wr=[SQk[db]])
        for fn in deferred:
            fn()
        post_norm_residual(l, PV_MIXPOST)

    NCK = 2 * NC_
    assert NCK * 16 <= 64

    def ssd_prep(l):
        pbb = l * 48
        n = NCK * 16
        PS0 = next_proj()
        for ci in range(NCK):
            sl = slice(ci * 128, (ci + 1) * 128)
            for k in range(KD):
                op("pe", lambda e, k=k: e.matmul(PS0.h[:, ci * 16:(ci + 1) * 16], lhsT=H.h[:, k, sl], rhs=WDT.h[:, l, k * 16:(k + 1) * 16],
                                                 start=(k == 0), stop=(k == KD - 1)), rd=[Hk[k], WDT], wr=[PS0], inc=(k == KD - 1))

        def c3(ap):
            return ap.rearrange("p (c h) -> p c h", h=16)

        def b3(ap16):
            return ap16.unsqueeze(1).to_broadcast([128, NCK, 16])
        op("dve", lambda e: e.tensor_tensor(out=c3(T0.h[:, 0:n]), in0=c3(PS0.h[:, 0:n]), in1=b3(PBt.h[:, pbb:pbb + 16]), op=ALU.add), rd=[PS0, PBt], wr=[T0])
        op("dve", lambda e: e.tensor_scalar(out=Mx.h[:, 0:n], in0=T0.h[:, 0:n], scalar1=0.0, scalar2=None, op0=ALU.max), rd=[T0], wr=[Mx])
        op("dve", lambda e: e.scalar_tensor_tensor(out=NA.h[:, 0:n], in0=T0.h[:, 0:n], scalar=0.0, in1=Mx.h[:, 0:n], op0=ALU.min, op1=ALU.subtract),
           rd=[T0, Mx], wr=[NA])
        op("act", lambda e: e.activation(out=NA.h[:, 0:n], in_=NA.h[:, 0:n], func=AF.Exp), rd=[NA], wr=[NA])
        op("act", lambda e: e.activation(out=NA.h[:, 0:n], in_=NA.h[:, 0:n], func=AF.Ln, bias=CST.h[:, 0:1], scale=1.0), rd=[NA, CST], wr=[NA])
        op("dve", lambda e: e.tensor_tensor(out=DT.h[:, 0:n], in0=Mx.h[:, 0:n], in1=NA.h[:, 0:n], op=ALU.add), rd=[Mx, NA], wr=[DT])
        op("dve", lambda e: e.tensor_tensor(out=c3(ADT.h[:, 0:n]), in0=c3(DT.h[:, 0:n]), in1=b3(AB.h[:, l * 16:(l + 1) * 16]), op=ALU.mult), rd=[DT, AB], wr=[ADT])
        PS1 = next_proj()
        for ci in range(NCK):
            op("pe", lambda e: e.matmul(PS1.h[:, ci * 16:(ci + 1) * 16], lhsT=UT.h[:, :], rhs=ADT.h[:, ci * 16:(ci + 1) * 16], start=True, stop=True),
               rd=[UT, ADT], wr=[PS1], inc=False)
            op("pe", lambda e: e.matmul(PS1.h[:, 64 + ci * 16:64 + (ci + 1) * 16], lhsT=ONE32.h[:, :], rhs=ADT.h[:, ci * 16:(ci + 1) * 16], start=True, stop=True),
               rd=[ONE32, ADT], wr=[PS1], inc=(ci == NCK - 1))
        op("dve", lambda e: e.tensor_scalar(out=NCS.h[:, 0:n], in0=PS1.h[:, 0:n], scalar1=-1.0, scalar2=None, op0=ALU.mult), rd=[PS1], wr=[NCS])
        op("act", lambda e: e.activation(out=ECS.h[:, 0:n], in_=PS1.h[:, 0:n], func=AF.Exp), rd=[PS1], wr=[ECS])
        op("dve", lambda e: e.tensor_tensor(out=DTE.h[:, 0:n], in0=PS1.h[:, 64:64 + n], in1=NCS.h[:, 0:n], op=ALU.add), rd=[PS1, NCS], wr=[DTE])
        op("act", lambda e: e.activation(out=DTE.h[:, 0:n], in_=DTE.h[:, 0:n], func=AF.Exp), rd=[DTE], wr=[DTE])
        op("act", lambda e: e.activation(out=CD.h[:, 0:n], in_=PS1.h[:, 64:64 + n], func=AF.Exp), rd=[PS1], wr=[CD])
        op("dve", lambda e: e.tensor_tensor(out=DTDTE.h[:, 0:n], in0=DT.h[:, 0:n], in1=DTE.h[:, 0:n], op=ALU.mult), rd=[DT, DTE], wr=[DTDTE])

    def ssd_chunk(l, q, c, first, defer=None):
        ci = q * NC_ + c
        co = ci * 16
        sl = slice(ci * 128, (ci + 1) * 128)
        pbb = l * 48
        Sl = S[q][l]
        XDT, XDTE, XSD, DEC, MT, Y1, BTOK, SQC, SBF = XDTs[q], XDTEs[q], XSDs[q], DMs[q], DMs[q], Y1s[q], BTOKs[q], SQCs[q], SBFs[q]
        pbt = next_proj()
        for g in range(2):
            op("pe", lambda e, g=g: e.matmul(pbt.h[:, g * 128:(g + 1) * 128], lhsT=BT.h[:, g, sl], rhs=IBF.h[:, :], start=True, stop=True),
               rd=[BT, IBF], wr=[pbt], inc=(g == 1))
        op("act", lambda e: e.activation(out=BTOK.h[:, :, :], in_=pbt.h[:, 0:256].rearrange("p (g n) -> p g n", g=2), func=AF.Copy), rd=[pbt], wr=[BTOK])
        for g in range(2):
            px = next_proj()
            for jj in range(4):
                op("pe", lambda e, jj=jj: e.matmul(px.h[:, jj * 128:(jj + 1) * 128], lhsT=XS.h[:, 4 * g + jj, sl], rhs=IBF.h[:, :], start=True, stop=True),
                   rd=[XSk[4 * g + jj], IBF], wr=[px], inc=(jj == 3))
            op("dve", lambda e: e.tensor_tensor(out=v3(XDT[g].h[:, :]), in0=v3(px.h[:, :]), in1=bc16(DT, co + g * 8, 8), op=ALU.mult), rd=[px, DT], wr=[XDT[g]])
            op("dve", lambda e: e.tensor_tensor(out=v3(XDTE[g].h[:, :]), in0=v3(px.h[:, :]), in1=bc16(DTDTE, co + g * 8, 8), op=ALU.mult), rd=[px, DTDTE], wr=[XDTE[g]])
            op("dve", lambda e: e.tensor_tensor(out=v3(XSD[g].h[:, :]), in0=v3(px.h[:, :]),
                                                in1=PBt.h[:, pbb + 32 + g * 8:pbb + 40 + g * 8].unsqueeze(2).to_broadcast([128, 8, HP]), op=ALU.mult),
               rd=[px, PBt], wr=[XSD[g]])
        yield 4.0
        for g in range(2):
            dec, mt = DEC[g], MT[g]
            pgs = [next_proj(), next_proj()]
            for hh in range(8):
                h = co + g * 8 + hh
                pg = pgs[hh // 4]
                cs_ = slice((hh % 4) * 128, (hh % 4) * 128 + 128)
                op("pe", lambda e: e.matmul(pg.h[:, cs_], lhsT=ADT.h[:, h:h + 1].to_broadcast([128, 128]), rhs=UT.h[:, :], start=True, stop=False),
                   rd=[ADT, UT], wr=[pg], inc=False)
                if MASK_BF:
                    op("pe", lambda e: e.matmul(pg.h[:, cs_], lhsT=IBF.h[:, :], rhs=MASKB.h[:, :], start=False, stop=True),
                       rd=[IBF, MASKB], wr=[pg], inc=(hh % 4 == 3))
                else:
                    op("pe", lambda e: e.matmul(pg.h[:, cs_], lhsT=I32.h[:, :], rhs=MASK32.h[:, :], start=False, stop=True),
                       rd=[I32, MASK32], wr=[pg], inc=(hh % 4 == 3))
            for hh in range(8):
                h = co + g * 8 + hh
                pg = pgs[hh // 4]
                cs_ = slice((hh % 4) * 128, (hh % 4) * 128 + 128)
                op("act", lambda e: e.activation(out=dec.h[:, hh, :], in_=pg.h[:, cs_], func=AF.Exp, bias=NCS.h[:, h:h + 1], scale=1.0),
                   rd=[pg, NCS], wr=[dec])
            psc = next_proj()
            op("pe", lambda e: e.matmul(psc.h[:, 0:128], lhsT=BT.h[:, g, sl], rhs=CT.h[:, g, sl], start=True, stop=True), rd=[BT, CT], wr=[psc])
            op("dve", lambda e: e.tensor_tensor(out=mt.h[:, :, :], in0=dec.h[:, :, :], in1=psc.h[:, 0:128].unsqueeze(1).to_broadcast([128, 8, 128]), op=ALU.mult),
               rd=[dec, psc], wr=[mt])
            yield 3.0
        if not first:
            op("act", lambda e: e.activation(out=SBF.h[:, :], in_=Sl.h[:, :], func=AF.Copy), rd=[Sl], wr=[SBF])
        pys, pggs, pts, pgns = [], [], [], []
        for g in range(2):
            py = next_proj()
            pys.append(py)
            op("pe", lambda e: e.matmul(py.h[:, :], lhsT=IBF.h[:, :], rhs=XSD[g].h[:, :], start=True, stop=False), rd=[IBF, XSD[g]], wr=[py], inc=False)
            for hh in range(8):
                op("pe", lambda e, hh=hh: e.matmul(py.h[:, hh * HP:(hh + 1) * HP], lhsT=MT[g].h[:, hh, :], rhs=XDT[g].h[:, hh * HP:(hh + 1) * HP],
                                                   start=False, stop=(hh == 7)), rd=[MT[g], XDT[g]], wr=[py], inc=(hh == 7))
            if not first:
                pgg = next_proj()
                pggs.append(pgg)
                op("pe", lambda e: e.matmul(pgg.h[:, :], lhsT=CT.h[:, g, sl], rhs=SBF.h[:, g * 512:(g + 1) * 512], start=True, stop=True),
                   rd=[CT, SBF], wr=[pgg])
        YT = [rot(SCR), rot(SCR)]
        for g in range(2):
            if not first:
                op("dve", lambda e: e.tensor_tensor(out=v3(YT[g].h[:, :]), in0=v3(pggs[g].h[:, :]), in1=bc16(ECS, co + g * 8, 8), op=ALU.mult),
                   rd=[pggs[g], ECS], wr=[YT[g]])
                op("dve", lambda e: e.tensor_tensor(out=Y1[g].h[:, :], in0=pys[g].h[:, :], in1=YT[g].h[:, :], op=ALU.add), rd=[pys[g], YT[g]], wr=[Y1[g]])
            else:
                op("act", lambda e: e.activation(out=Y1[g].h[:, :], in_=pys[g].h[:, :], func=AF.Copy), rd=[pys[g]], wr=[Y1[g]])
        yield 4.0
        for g in range(2):
            pt = next_proj()
            pts.append(pt)
            for jj in range(4):
                op("pe", lambda e, jj=jj: e.matmul(pt.h[:, jj * 128:(jj + 1) * 128], lhsT=Y1[g].h[:, jj * 128:(jj + 1) * 128], rhs=IBF.h[:, :], start=True, stop=True),
                   rd=[Y1[g], IBF], wr=[pt], inc=(jj == 3))
        for g in range(2):
            op("dve", lambda e: e.tensor_tensor(out=YS.h[:, 4 * g:4 * g + 4, sl], in0=pts[g].h[:, :].rearrange("p (j t) -> p j t", t=128),
                                                in1=SZ.h[:, 4 * g:4 * g + 4, sl], op=ALU.mult), rd=[pts[g]] + SZk[4 * g:4 * g + 4], wr=YSk[4 * g:4 * g + 4])
            op("act", lambda e: e.activation(out=SQC[g].h[:, :, :], in_=YS.h[:, 4 * g:4 * g + 4, sl], func=AF.Square), rd=YSk[4 * g:4 * g + 4], wr=[SQC[g]])
        pgn = next_proj()
        for g in range(2):
            for kk in range(4):
                op("pe", lambda e, kk=kk: e.matmul(pgn.h[:, g * 128:(g + 1) * 128], lhsT=ONESG.h[:, :], rhs=SQC[g].h[:, kk, :], start=(kk == 0), stop=(kk == 3)),
                   rd=[ONESG, SQC[g]], wr=[pgn], inc=(kk == 3))
        op("act", lambda e: e.activation(out=RSS.h[:, :, sl], in_=pgn.h[:, 0:256].rearrange("p (g t) -> p g t", g=2), func=AF.Copy), rd=[pgn], wr=[RSS])
        yield 3.0
        def state_update():
            for g in range(2):
                pst = next_proj()
                op("pe", lambda e: e.matmul(pst.h[:, :], lhsT=BTOK.h[:, g, :], rhs=XDTE[g].h[:, :], start=True, stop=True), rd=[BTOK, XDTE[g]], wr=[pst])
                if first:
                    op("act", lambda e: e.activation(out=Sl.h[:, g * 512:(g + 1) * 512], in_=pst.h[:, :], func=AF.Copy), rd=[pst], wr=[Sl])
                else:
                    op("dve", lambda e: e.tensor_tensor(out=v3(Sl.h[:, g * 512:(g + 1) * 512]), in0=v3(Sl.h[:, g * 512:(g + 1) * 512]),
                                                         in1=bc16(CD, co + g * 8, 8), op=ALU.mult), rd=[Sl, CD], wr=[Sl])
                    op("dve", lambda e: e.tensor_tensor(out=Sl.h[:, g * 512:(g + 1) * 512], in0=pst.h[:, :], in1=Sl.h[:, g * 512:(g + 1) * 512], op=ALU.add),
                       rd=[pst, Sl], wr=[Sl])
        if defer is None:
            state_update()
        else:
            defer.append(state_update)
        yield 2.0

    def mlp(l):
        norm_to_H(l, PV_MLPPRE, have_sq=True)
        for fb in range(32):
            w = next_w(l, CH_UP + fb)
            p = next_proj()
            proj_fm(p, w, H, Hk)
            r = rot(RR)
            op("act", lambda e: e.activation(out=r.h[:, :], in_=p.h[:, 0:W], func=AF.Relu), rd=[p], wr=[r])
            eng = "dve"
            op(eng, lambda e: e.tensor_tensor(out=FT.h[:, fb, :], in0=r.h[:, :], in1=r.h[:, :], op=ALU.mult), rd=[r], wr=[FTk[fb]])
        for db in range(KD):
            p = next_proj()
            for q in range(4):
                w = next_w(l, CH_DOWN + 4 * db + q)
                proj_fm(p, w, FT, FTk, k0=8 * q, start=(q == 0), stop=(q == 3))
            op("act", lambda e: e.activation(out=MIX.h[:, db, :], in_=p.h[:, 0:W], func=AF.Copy), rd=[p], wr=[MIXk[db]])
            op("act", lambda e: e.activation(out=SQ8.h[:, db, :], in_=p.h[:, 0:W], func=AF.Square), rd=[p], wr=[SQk[db]])
        post_norm_residual(l, PV_MLPPOST, final=(l == NL - 1))

    xv = xT_d.rearrange("(k p) n -> p k n", p=128)
    yv = yT_d.rearrange("(k p) n -> p k n", p=128)

    def whole():
        for t in range(NT):
            for q in range(2):
                tok0 = q * SEQ + t * TT
                dma(X.h[:, :, q * TT:(q + 1) * TT], xv[:, :, tok0:tok0 + TT], wr=Xk)
            for l in range(NL):
                mixer(l, t == 0, l > 0)
                mlp(l)
            for q in range(2):
                tok0 = q * SEQ + t * TT
                dma(yv[:, :, tok0:tok0 + TT], MIX.h[:, :, q * TT:(q + 1) * TT], rd=MIXk)

    P.dry = True
    whole()
    P.dry = False
    state.update({"proj": 0, "wnext": 0, "wissued": 0, "rr": 0})
    whole()
    assert state["wnext"] == len(worder)
    P.finish()
    return nc, P


def _chunk(wmat):
    return np.ascontiguousarray(wmat.reshape(KD, 128, 128).transpose(1, 0, 2)).reshape(128, 1024)


def prep_weights(w_in, w_out, w_up, w_down, NL):
    wch = np.empty((NL, NCH, 128, 1024), np.float32)
    wdt = np.empty((NL, 128, 128), np.float32)
    for l in range(NL):
        for j in range(KD):
            wch[l, 3 * j + 0] = _chunk(w_in[l][:, j * 128:(j + 1) * 128])
            wch[l, 3 * j + 1] = _chunk(w_in[l][:, 1024 + j * 128:1024 + (j + 1) * 128])
            wch[l, 3 * j + 2] = _chunk(w_in[l][:, 2048 + j * 128:2048 + (j + 1) * 128])
            wch[l, CH_Z + j] = _chunk(w_in[l][:, 3072 + j * 128:3072 + (j + 1) * 128])
        for j in range(12):
            wch[l, CH_XBC + j] = _chunk(w_in[l][:, 4096 + j * 128:4096 + (j + 1) * 128])
        for db in range(KD):
            for half in range(2):
                wch[l, CH_OUT + 2 * db + half] = _chunk(w_out[l][half * 1024:(half + 1) * 1024, db * 128:(db + 1) * 128])
            for q in range(4):
                wch[l, CH_DOWN + 4 * db + q] = _chunk(w_down[l][q * 1024:(q + 1) * 1024, db * 128:(db + 1) * 128])
        for fb in range(32):
            wch[l, CH_UP + fb] = _chunk(w_up[l][:, fb * 128:(fb + 1) * 128])
        wdt[l] = w_in[l][:, 5632:5648].reshape(KD, 128, 16).transpose(1, 0, 2).reshape(128, 128)
    return wch, wdt


def prep_params(inp, NL):
    pv = np.empty((128, NL, NPV), np.float32)
    pb = np.empty((128, NL, 48), np.float32)

    def cols(v, nb):
        v = np.asarray(v, np.float32)
        lead = v.shape[:-1]
        return np.moveaxis(v.reshape(lead + (nb, 128)), -1, 0).reshape(128, -1)

    for l in range(NL):
        pv[:, l, PV_MIXPRE:PV_MIXPRE + 8] = cols(inp["norm_mix_pre"][l], 8)
        pv[:, l, PV_CVA:PV_CVA + 24] = cols(inp["conv_a_w"][l], 8)
        pv[:, l, PV_CVS:PV_CVS + 48] = cols(inp["ssm_conv_w"][l], 12)
        pv[:, l, PV_CVB:PV_CVB + 12] = cols(inp["ssm_conv_b"][l], 12)
        pv[:, l, PV_MIXPOST:PV_MIXPOST + 8] = cols(inp["norm_mix_post"][l], 8)
        pv[:, l, PV_MLPPRE:PV_MLPPRE + 8] = cols(inp["norm_mlp_pre"][l], 8)
        pv[:, l, PV_MLPPOST:PV_MLPPOST + 8] = cols(inp["norm_mlp_post"][l], 8)
        pv[:, l, PV_GA:PV_GA + 8] = cols(inp["conv_out_norm"][l], 8)
        pv[:, l, PV_GS:PV_GS + 8] = cols(inp["ssm_out_norm"][l], 8)
        pb[:, l, 0:16] = np.asarray(inp["dt_bias"][l], np.float32)[None, :]
        pb[:, l, 16:32] = np.asarray(inp["a_log"][l], np.float32)[None, :]
        pb[:, l, 32:48] = np.asarray(inp["d_skip"][l], np.float32)[None, :]
    return pv.reshape(128, NL * NPV), pb.reshape(128, NL * 48)


def run(inp, NL=4, NB=2, SEQ=2048, TT=256, n_cores=8, trace=False):
    x = np.asarray(inp["x"], np.float32)
    wch, wdt = prep_weights(np.asarray(inp["w_in"], np.float32), np.asarray(inp["w_out"], np.float32),
                            np.asarray(inp["w_up"], np.float32), np.asarray(inp["w_down"], np.float32), NL)
    pv, pb = prep_params(inp, NL)
    nc, P = build_program(NL=NL, NB=NB, SEQ=SEQ, TT=TT)
    in_maps = []
    for c in range(n_cores):
        xs = x[c * NB:(c + 1) * NB].reshape(NB * SEQ, D)
        in_maps.append({"xT": np.ascontiguousarray(xs.T), "wch": wch, "wdt": wdt, "pv": pv, "pb": pb})
    res = run_bass_kernel_spmd(nc, in_maps, core_ids=list(range(n_cores)), trace=trace)
    out = np.empty((n_cores * NB, SEQ, D), np.float32)
    for c in range(n_cores):
        out[c * NB:(c + 1) * NB] = np.ascontiguousarray(res.results[c]["yT"].T).reshape(NB, SEQ, D)
    return out, res


def kernel(**inputs):
    out, _ = run(inputs)
    return out
```

```python
import numpy as np
import concourse.bass as bass
import concourse.mybir as mybir
from concourse.bass_utils import run_bass_kernel_spmd

F32 = mybir.dt.float32
BF16 = mybir.dt.bfloat16
ALU = mybir.AluOpType
AF = mybir.ActivationFunctionType


class Tile:
    __slots__ = ("h", "name", "w", "r", "psum")

    def __init__(self, h, name, psum=False):
        self.h = h
        self.name = name
        self.w = None
        self.r = {}
        self.psum = psum


class Prog:
    NDMA = 24

    def __init__(self, nc):
        self.nc = nc
        self.eng = {"pe": nc.tensor, "dve": nc.vector, "act": nc.scalar, "pool": nc.gpsimd, "sp": nc.sync}
        self.sem = {}
        self.cnt = {}
        for e in ("pe", "dve", "act", "pool"):
            self.sem[e] = nc.alloc_semaphore("s_" + e)
            self.cnt[e] = 0
        for i in range(self.NDMA):
            k = "d%d" % i
            self.sem[k] = nc.alloc_semaphore("s_" + k)
            self.cnt[k] = 0
        self.seen = {e: {} for e in self.eng}
        self.dma_rr = 0
        self.n_inst = 0
        self.n_wait = 0
        self.dry = False
        self.mark_half = False

    def sb(self, name, shape, dtype):
        return Tile(self.nc.alloc_sbuf_tensor(name, list(shape), dtype), name)

    def ps(self, name, shape, dtype):
        return Tile(self.nc.alloc_psum_tensor(name, list(shape), dtype), name, psum=True)

    def need(self, eng, ev):
        key, val = ev
        if self.seen[eng].get(key, 0) >= val:
            return
        assert val <= self.cnt[key], "wait on pending event %s %d > %d" % (key, val, self.cnt[key])
        self.eng[eng].wait_ge(self.sem[key], val)
        self.seen[eng][key] = val
        self.n_wait += 1

    def _deps(self, eng, rd, wr):
        for t in rd:
            if t.w is not None:
                if not (t.w[0] == eng and eng == "pe"):
                    self.need(eng, t.w)
            if t.psum:
                for k, v in t.r.items():
                    if k != eng:
                        self.need(eng, (k, v))
        for t in wr:
            if t.w is not None and not (t.w[0] == eng and eng == "pe"):
                self.need(eng, t.w)
            for k, v in t.r.items():
                if not (k == eng and eng == "pe"):
                    self.need(eng, (k, v))

    def op(self, eng, fn, rd=(), wr=(), inc=True):
        if self.dry:
            return None
        self._deps(eng, rd, wr)
        inst = fn(self.eng[eng])
        val = self.cnt[eng] + 1
        if inc:
            inst.then_inc(self.sem[eng], 1)
            self.cnt[eng] = val
        else:
            assert eng == "pe"
        for t in rd:
            if t.r.get(eng, 0) < val:
                t.r[eng] = val
        for t in wr:
            t.w = (eng, val)
            t.r = {}
        self.n_inst += 1
        return inst

    def dma(self, out_ap, in_ap, rd=(), wr=(), queue="sp"):
        if self.dry:
            return None
        k = "d%d" % self.dma_rr
        self.dma_rr = (self.dma_rr + 1) % self.NDMA
        if self.cnt[k]:
            self.need(queue, (k, self.cnt[k]))
        self._deps(queue, rd, wr)
        inst = self.eng[queue].dma_start(out=out_ap, in_=in_ap)
        inst.then_inc(self.sem[k], 16)
        self.cnt[k] += 16
        val = self.cnt[k]
        for t in rd:
            t.r[k] = val
        for t in wr:
            t.w = (k, val)
            t.r = {}
        self.n_inst += 1
        return inst

    def handoff(self, src, dst):
        if self.dry:
            return
        ev = {}
        for t in src:
            if t.w is not None:
                ev[t.w[0]] = max(ev.get(t.w[0], 0), t.w[1])
            for k, v in t.r.items():
                ev[k] = max(ev.get(k, 0), v)
        for d in dst:
            for k, v in ev.items():
                if d.r.get(k, 0) < v:
                    d.r[k] = v

    def barrier(self, engines=None):
        for e in (engines or self.eng):
            for k, v in self.cnt.items():
                if v and k != e:
                    self.need(e, (k, v))

    def finish(self):
        self.barrier(["sp"])


D = 1024
KD = 8
NHEAD = 16
HP = 64
DFF = 4096
EPS = 1e-6
NCH = 124
CH_Z = 24
CH_XBC = 32
CH_OUT = 44
CH_UP = 60
CH_DOWN = 92
NPV = 132
PV_MIXPRE, PV_CVA, PV_CVS, PV_CVB, PV_MIXPOST, PV_MLPPRE, PV_MLPPOST, PV_GA, PV_GS = 0, 8, 32, 80, 92, 100, 108, 116, 124
MASKNEG = -30000.0
MASK_BF = True
SSD_ACC = True


def build_program(NL=4, NB=2, SEQ=2048, TT=256, NWB=7):
    assert NB == 2, "two interleaved streams = the two sequences owned by this core"
    NTOK = NB * SEQ
    NT = SEQ // TT
    NC_ = TT // 128
    nc = bass.Bass("TRN2", target_bir_lowering=False)
    xT_d = nc.dram_tensor("xT", [D, NTOK], F32, kind="ExternalInput").ap()
    wch_d = nc.dram_tensor("wch", [NL, NCH, 128, 1024], F32, kind="ExternalInput").ap()
    wdt_d = nc.dram_tensor("wdt", [NL, 128, 128], F32, kind="ExternalInput").ap()
    pv_d = nc.dram_tensor("pv", [128, NL * NPV], F32, kind="ExternalInput").ap()
    pb_d = nc.dram_tensor("pb", [128, NL * 48], F32, kind="ExternalInput").ap()
    yT_d = nc.dram_tensor("yT", [D, NTOK], F32, kind="ExternalOutput").ap()
    wsc_d = nc.dram_tensor("wsc", [NL, NCH, 128, 1024], BF16, kind="Internal").ap()

    P = Prog(nc)
    sb, ps, op, dma = P.sb, P.ps, P.op, P.dma

    PV = sb("PV", [128, NL * NPV], F32)
    PBt = sb("PBt", [128, NL * 48], F32)
    AB = sb("AB", [128, NL * 16], F32)
    WDT = sb("WDT", [128, NL, 128], BF16)
    I32 = sb("I32", [128, 128], F32)
    IBF = sb("IBF", [128, 128], BF16)
    UT = sb("UT", [128, 128], F32)
    UTB = sb("UTB", [128, 128], BF16)
    ONE32 = sb("ONE32", [128, 128], F32)
    MASKB = sb("MASKB", [128, 128], BF16)
    MASK32 = sb("MASK32", [128, 128], F32)
    ONESD = sb("ONESD", [128, 128], BF16)
    ONESG = sb("ONESG", [128, 128], BF16)
    CST = sb("CST", [128, 2], F32)

    dma(PV.h[:, :], pv_d, wr=[PV])
    dma(PBt.h[:, :], pb_d, wr=[PBt])
    op("pool", lambda e: e.memset(I32.h[:, :], 1.0), wr=[I32])
    op("pool", lambda e: e.affine_select(out=I32.h[:, :], in_=I32.h[:, :], pattern=[[1, 128]], compare_op=ALU.is_equal,
                                         fill=0.0, base=0, channel_multiplier=-1), rd=[I32], wr=[I32])
    op("pool", lambda e: e.memset(UT.h[:, :], 1.0), wr=[UT])
    op("pool", lambda e: e.affine_select(out=UT.h[:, :], in_=UT.h[:, :], pattern=[[1, 128]], compare_op=ALU.is_ge,
                                         fill=0.0, base=0, channel_multiplier=-1), rd=[UT], wr=[UT])
    op("pool", lambda e: e.memset(MASK32.h[:, :], 0.0), wr=[MASK32])
    op("pool", lambda e: e.affine_select(out=MASK32.h[:, :], in_=MASK32.h[:, :], pattern=[[1, 128]], compare_op=ALU.is_ge,
                                         fill=MASKNEG, base=0, channel_multiplier=-1), rd=[MASK32], wr=[MASK32])
    op("dve", lambda e: e.tensor_copy(out=MASKB.h[:, :], in_=MASK32.h[:, :]), rd=[MASK32], wr=[MASKB])
    op("pool", lambda e: e.memset(ONE32.h[:, :], 1.0), wr=[ONE32])
    op("pool", lambda e: e.memset(ONESD.h[:, :], 1.0 / D), wr=[ONESD])
    op("pool", lambda e: e.memset(ONESG.h[:, :], 1.0 / 512), wr=[ONESG])
    op("pool", lambda e: e.memset(CST.h[:, 0:1], 1.0), wr=[CST])
    op("pool", lambda e: e.memset(CST.h[:, 1:2], EPS), wr=[CST])
    op("dve", lambda e: e.tensor_copy(out=IBF.h[:, :], in_=I32.h[:, :]), rd=[I32], wr=[IBF])
    op("dve", lambda e: e.tensor_copy(out=UTB.h[:, :], in_=UT.h[:, :]), rd=[UT], wr=[UTB])
    for l in range(NL):
        op("act", lambda e, l=l: e.activation(out=AB.h[:, l * 16:(l + 1) * 16], in_=PBt.h[:, l * 48 + 16:l * 48 + 32], func=AF.Exp),
           rd=[PBt], wr=[AB])
    op("dve", lambda e: e.tensor_scalar(out=AB.h[:, :], in0=AB.h[:, :], scalar1=-1.0, scalar2=None, op0=ALU.mult), rd=[AB], wr=[AB])

    NSTG = 6
    stg_h = [nc.sbuf_tensor("STG%d" % i, [128, 2, 1024], F32) for i in range(NSTG)]
    stb_h = [nc.sbuf_tensor("STB%d" % i, [128, 2, 1024], BF16) for i in range(NSTG)]
    wds_h = nc.sbuf_tensor("WDS", [128, 128], F32)
    guards = stg_h + stb_h + [wds_h]
    handles = [g.__enter__() for g in guards]
    STG = [Tile(h, "STG") for h in handles[:NSTG]]
    STB = [Tile(h, "STB") for h in handles[NSTG:2 * NSTG]]
    WDS = Tile(handles[-1], "WDS")
    cast_rr = 0
    for l in range(NL):
        dma(WDS.h[:, :], wdt_d[l], wr=[WDS])
        op("dve", lambda e, l=l: e.tensor_copy(out=WDT.h[:, l, :], in_=WDS.h[:, :]), rd=[WDS], wr=[WDT])
    jobs = [(l, c0) for l in range(NL) for c0 in range(0, NCH, 2)]
    LOOKAHEAD = NSTG - 2

    def load(i):
        l, c0 = jobs[i]
        stg = STG[i % NSTG]
        dma(stg.h[:, :, :], wch_d[l, c0:c0 + 2].rearrange("c p f -> p c f"), wr=[stg])

    for i in range(min(LOOKAHEAD, len(jobs))):
        load(i)
    for i, (l, c0) in enumerate(jobs):
        stg, stb = STG[i % NSTG], STB[i % NSTG]
        if CH_OUT <= c0 < CH_UP:
            for cc in range(2):
                half = (c0 + cc - CH_OUT) % 2
                base = l * NPV + (PV_GA if half == 0 else PV_GS)
                for k in range(KD):
                    if k % 2 == 0:
                        op("dve", lambda e, cc=cc, k=k, base=base, stg=stg, stb=stb: e.tensor_scalar(
                            out=stb.h[:, cc, k * 128:(k + 1) * 128], in0=stg.h[:, cc, k * 128:(k + 1) * 128],
                            scalar1=PV.h[:, base + k:base + k + 1], scalar2=None, op0=ALU.mult), rd=[stg, PV], wr=[stb])
                    else:
                        op("act", lambda e, cc=cc, k=k, base=base, stg=stg, stb=stb: e.activation(
                            out=stb.h[:, cc, k * 128:(k + 1) * 128], in_=stg.h[:, cc, k * 128:(k + 1) * 128], func=AF.Copy,
                            scale=PV.h[:, base + k:base + k + 1]), rd=[stg, PV], wr=[stb])
        else:
            eng = ("dve", "act")[cast_rr % 2]
            cast_rr += 1
            if eng == "act":
                op("act", lambda e, stg=stg, stb=stb: e.activation(out=stb.h[:, :, :], in_=stg.h[:, :, :], func=AF.Copy), rd=[stg], wr=[stb])
            else:
                op(eng, lambda e, stg=stg, stb=stb: e.tensor_copy(out=stb.h[:, :, :], in_=stg.h[:, :, :]), rd=[stg], wr=[stb])
        if i + LOOKAHEAD < len(jobs):
            load(i + LOOKAHEAD)
        dma(wsc_d[l, c0:c0 + 2].rearrange("c p f -> p c f"), stb.h[:, :, :], rd=[stb])
    P.barrier()
    for g in reversed(guards):
        g.__exit__(None, None, None)

    W = 2 * TT
    X = sb("X", [128, KD, W], F32)
    H = sb("H", [128, KD, W], BF16)
    BIG = sb("BIG", [128, 32, W], BF16)
    FT = BIG
    YA = Tile(BIG.h[:, 0:8, :], "YA")
    YS = Tile(BIG.h[:, 8:16, :], "YS")
    XS = Tile(BIG.h[:, 16:24, :], "XS")
    SZ = Tile(BIG.h[:, 24:32, :], "SZ")
    BT = sb("BT", [128, 2, W], BF16)
    CT = sb("CT", [128, 2, W], BF16)
    MIX = sb("MIX", [128, KD, W], F32)
    SQ8 = sb("SQ8", [128, KD, W], BF16)
    RS = sb("RS", [128, W], F32)
    RSA = sb("RSA", [128, W], F32)
    RSS = sb("RSS", [128, 2, W], F32)
    S = [[sb("S%d_%d" % (q, l), [128, 1024], F32) for l in range(NL)] for q in range(2)]
    HA = [sb("HA%d" % l, [128, KD, 2, 2], F32) for l in range(NL)]
    HS = [sb("HS%d" % l, [128, 12, 2, 3], F32) for l in range(NL)]

    UU = [sb("UU%d" % i, [128, 2, TT + 3], F32) for i in range(4)]
    SCR = [sb("SCR%d" % i, [128, W], F32) for i in range(8)]
    UA = US = UU
    VV = XAt = RR = TMP = SCR
    XDTs = [[sb("XDT%d_%d" % (q, i), [128, 512], BF16) for i in range(2)] for q in range(2)]
    XDTEs = [[sb("XDTE%d_%d" % (q, i), [128, 512], BF16) for i in range(2)] for q in range(2)]
    XSDs = [[sb("XSD%d_%d" % (q, i), [128, 512], BF16) for i in range(2)] for q in range(2)]
    DMs = [[sb("DM%d_%d" % (q, i), [128, 8, 128], BF16) for i in range(2)] for q in range(2)]
    Y1s = [[sb("Y1%d_%d" % (q, i), [128, 512], BF16) for i in range(2)] for q in range(2)]
    BTOKs = [sb("BTOK%d" % q, [128, 2, 128], BF16) for q in range(2)]
    SQCs = [[sb("SQC%d_%d" % (q, i), [128, 4, 128], BF16) for i in range(2)] for q in range(2)]
    SBFs = [sb("SBF%d" % q, [128, 1024], BF16) for q in range(2)]
    T0 = sb("T0", [128, 64], F32)
    Mx = sb("Mx", [128, 64], F32)
    NA = sb("NA", [128, 64], F32)
    DT = sb("DT", [128, 64], F32)
    ADT = sb("ADT", [128, 64], F32)
    NCS = sb("NCS", [128, 64], F32)
    ECS = sb("ECS", [128, 64], F32)
    DTE = sb("DTE", [128, 64], F32)
    CD = sb("CD", [128, 64], F32)
    DTDTE = sb("DTDTE", [128, 64], F32)
    WB = [sb("WB%d" % i, [128, KD, 128], BF16) for i in range(NWB)]

    BANKS = [ps("BK%d" % i, [128, 512], F32) for i in range(8)]
    Xk = [Tile(None, "Xk%d" % k) for k in range(KD)]
    Hk = [Tile(None, "Hk%d" % k) for k in range(KD)]
    MIXk = [Tile(None, "MIXk%d" % k) for k in range(KD)]
    SQk = [Tile(None, "SQk%d" % k) for k in range(KD)]
    Bk = [Tile(None, "Bk%d" % k) for k in range(32)]
    YAk, YSk, XSk, SZk, FTk = Bk[0:8], Bk[8:16], Bk[16:24], Bk[24:32], Bk
    state = {"proj": 0, "wnext": 0, "wissued": 0, "rr": 0}
    worder = []

    def issue_weights(upto):
        while state["wissued"] < min(upto, len(worder)):
            g = state["wissued"]
            l, c = worder[g]
            wb = WB[g % NWB]
            dma(wb.h[:, :, :], wsc_d[l, c].rearrange("p (k e) -> p k e", k=KD), wr=[wb])
            state["wissued"] += 1

    def next_w(l, c):
        if P.dry:
            worder.append((l, c))
            return WB[0]
        g = state["wnext"]
        assert worder[g] == (l, c), (worder[g], (l, c))
        issue_weights(g + NWB - 1)
        state["wnext"] += 1
        issue_weights(g + 1)
        return WB[g % NWB]

    def next_proj():
        p = BANKS[state["proj"] % 8]
        state["proj"] += 1
        return p

    def rot(lst):
        state["rr"] += 1
        return lst[state["rr"] % len(lst)]

    def s3(ap):
        return ap.rearrange("p (s t) -> p s t", s=2)

    def proj_fm(pt, wb, src, srck, k0=0, nk=KD, start=True, stop=True):
        for k in range(nk):
            last = (k == nk - 1)
            op("pe", lambda e, k=k: e.matmul(pt.h[:, 0:W], lhsT=wb.h[:, k, :], rhs=src.h[:, k0 + k, :],
                                             start=(start and k == 0), stop=(stop and last)),
               rd=[wb, srck[k0 + k]], wr=[pt], inc=last)

    def rstd_from(ps_ap, ps_tile, out_ap, out_tile):
        op("act", lambda e: e.activation(out=out_ap, in_=ps_ap, func=AF.Ln, bias=CST.h[:, 1:2], scale=1.0), rd=[ps_tile, CST], wr=[out_tile])
        op("act", lambda e: e.activation(out=out_ap, in_=out_ap, func=AF.Exp, scale=-0.5), rd=[out_tile], wr=[out_tile])

    def stats_mm(pt):
        for k in range(KD):
            op("pe", lambda e, k=k: e.matmul(pt.h[:, 0:W], lhsT=ONESD.h[:, :], rhs=SQ8.h[:, k, :], start=(k == 0), stop=(k == KD - 1)),
               rd=[ONESD, SQk[k]], wr=[pt], inc=(k == KD - 1))

    def norm_to_H(l, pvcol, have_sq=False):
        pt = next_proj()
        if not have_sq:
            op("act", lambda e: e.activation(out=SQ8.h[:, :, :], in_=X.h[:, :, :], func=AF.Square), rd=Xk, wr=SQk)
        stats_mm(pt)
        rstd_from(pt.h[:, 0:W], pt, RS.h[:, :], RS)
        for k in range(KD):
            c = l * NPV + pvcol + k
            op("dve", lambda e, k=k, c=c: e.scalar_tensor_tensor(out=H.h[:, k, :], in0=X.h[:, k, :], scalar=PV.h[:, c:c + 1], in1=RS.h[:, :],
                                                                 op0=ALU.mult, op1=ALU.mult), rd=[Xk[k], PV, RS], wr=[Hk[k]])

    def post_norm_residual(l, pvcol, final=False):
        pt = next_proj()
        stats_mm(pt)
        rstd_from(pt.h[:, 0:W], pt, RS.h[:, :], RS)
        tmps = [rot(TMP) for k in range(KD)]
        for k in range(KD):
            c = l * NPV + pvcol + k
            tmp = tmps[k]
            op("dve", lambda e, k=k, c=c, tmp=tmp: e.scalar_tensor_tensor(out=tmp.h[:, :], in0=MIX.h[:, k, :], scalar=PV.h[:, c:c + 1], in1=RS.h[:, :],
                                                                          op0=ALU.mult, op1=ALU.mult), rd=[MIXk[k], PV, RS], wr=[tmp])
        for k in range(KD):
            tmp = tmps[k]
            eng = "dve"
            if final:
                op(eng, lambda e, k=k, tmp=tmp: e.tensor_tensor(out=MIX.h[:, k, :], in0=X.h[:, k, :], in1=tmp.h[:, :], op=ALU.add), rd=[Xk[k], tmp], wr=[MIXk[k]])
                continue
            op(eng, lambda e, k=k, tmp=tmp: e.tensor_tensor(out=X.h[:, k, :], in0=X.h[:, k, :], in1=tmp.h[:, :], op=ALU.add), rd=[Xk[k], tmp], wr=[Xk[k]])
            op("act", lambda e, k=k: e.activation(out=SQ8.h[:, k, :], in_=X.h[:, k, :], func=AF.Square), rd=[Xk[k]], wr=[SQk[k]])

    def bc16(t, h0, nh):
        return t.h[:, h0:h0 + nh].unsqueeze(2).to_broadcast([128, nh, HP])

    def v3(ap):
        return ap.rearrange("p (h d) -> p h d", d=HP)

    def interleave(gens):
        vt = [0.0] * len(gens)
        alive = [True] * len(gens)
        while any(alive):
            i = min((k for k in range(len(gens)) if alive[k]), key=lambda k: vt[k])
            try:
                vt[i] += next(gens[i])
            except StopIteration:
                alive[i] = False

    def group_a_blocks(l, first_tile):
        pvb = l * NPV
        for j in range(KD):
            w1 = next_w(l, 3 * j)
            p1 = next_proj()
            proj_fm(p1, w1, H, Hk)
            w2 = next_w(l, 3 * j + 1)
            p2 = next_proj()
            proj_fm(p2, w2, H, Hk)
            w3 = next_w(l, 3 * j + 2)
            p3 = next_proj()
            proj_fm(p3, w3, H, Hk)
            xa, u, v = rot(XAt), rot(UA), rot(VV)
            op("act", lambda e: e.activation(out=xa.h[:, :], in_=p1.h[:, 0:W], func=AF.Copy), rd=[p1], wr=[xa])
            if first_tile:
                op("pool", lambda e: e.memset(u.h[:, :, 1:3], 0.0), wr=[u])
            else:
                op("act", lambda e: e.activation(func=AF.Copy, out=u.h[:, :, 1:3], in_=HA[l].h[:, j, :, :]), rd=[HA[l]], wr=[u])
            op("dve", lambda e: e.tensor_tensor(out=u.h[:, :, 3:3 + TT], in0=s3(p2.h[:, 0:W]), in1=s3(xa.h[:, :]), op=ALU.mult), rd=[p2, xa], wr=[u])
            op("act", lambda e: e.activation(func=AF.Copy, out=HA[l].h[:, j, :, :], in_=u.h[:, :, TT + 1:TT + 3]), rd=[u], wr=[HA[l]])
            c0 = pvb + PV_CVA + j
            op("act", lambda e: e.activation(out=s3(v.h[:, :]), in_=u.h[:, :, 3:3 + TT], func=AF.Copy, scale=PV.h[:, c0 + 16:c0 + 17]), rd=[u, PV], wr=[v])
            op("dve", lambda e: e.scalar_tensor_tensor(out=s3(v.h[:, :]), in0=u.h[:, :, 2:2 + TT], scalar=PV.h[:, c0 + 8:c0 + 9], in1=s3(v.h[:, :]),
                                                       op0=ALU.mult, op1=ALU.add), rd=[u, PV, v], wr=[v])
            op("dve", lambda e: e.scalar_tensor_tensor(out=s3(v.h[:, :]), in0=u.h[:, :, 1:1 + TT], scalar=PV.h[:, c0:c0 + 1], in1=s3(v.h[:, :]),
                                                       op0=ALU.mult, op1=ALU.add), rd=[u, PV, v], wr=[v])
            op("dve", lambda e: e.tensor_tensor(out=YA.h[:, j, :], in0=p3.h[:, 0:W], in1=v.h[:, :], op=ALU.mult), rd=[p3, v], wr=[YAk[j]])
            op("act", lambda e: e.activation(out=SQ8.h[:, j, :], in_=YA.h[:, j, :], func=AF.Square), rd=[YAk[j]], wr=[SQk[j]])
            yield 3.0

    def mixer(l, first_tile, have_sq):
        pvb = l * NPV
        norm_to_H(l, PV_MIXPRE, have_sq=have_sq)
        def z_block(j):
            w = next_w(l, CH_Z + j)
            p = next_proj()
            proj_fm(p, w, H, Hk)
            op("act", lambda e: e.activation(out=SZ.h[:, j, :], in_=p.h[:, 0:W], func=AF.Silu), rd=[p], wr=[SZk[j]])
        zj = [0]
        for j0 in range(0, 12, 2):
            js = (j0, j0 + 1)
            ps_, us_, vs_ = [], [], []
            for j in js:
                w = next_w(l, CH_XBC + j)
                p = next_proj()
                proj_fm(p, w, H, Hk)
                ps_.append(p)
                us_.append(rot(US))
                vs_.append(rot(VV))
            for i, j in enumerate(js):
                u = us_[i]
                if first_tile:
                    op("pool", lambda e: e.memset(u.h[:, :, 0:3], 0.0), wr=[u])
                else:
                    op("act", lambda e: e.activation(func=AF.Copy, out=u.h[:, :, 0:3], in_=HS[l].h[:, j, :, :]), rd=[HS[l]], wr=[u])
                op("act", lambda e: e.activation(out=u.h[:, :, 3:3 + TT], in_=s3(ps_[i].h[:, 0:W]), func=AF.Copy), rd=[ps_[i]], wr=[u])
                op("act", lambda e: e.activation(func=AF.Copy, out=HS[l].h[:, j, :, :], in_=u.h[:, :, TT:TT + 3]), rd=[u], wr=[HS[l]])
            for i, j in enumerate(js):
                u, v = us_[i], vs_[i]
                c0 = pvb + PV_CVS + j
                cb = pvb + PV_CVB + j
                op("dve", lambda e: e.tensor_scalar(out=s3(v.h[:, :]), in0=u.h[:, :, 3:3 + TT], scalar1=PV.h[:, c0 + 36:c0 + 37], scalar2=PV.h[:, cb:cb + 1],
                                                    op0=ALU.mult, op1=ALU.add), rd=[u, PV], wr=[v])
            for kk in (2, 1, 0):
                for i, j in enumerate(js):
                    u, v = us_[i], vs_[i]
                    c0 = pvb + PV_CVS + j
                    op("dve", lambda e: e.scalar_tensor_tensor(out=s3(v.h[:, :]), in0=u.h[:, :, kk:kk + TT], scalar=PV.h[:, c0 + 12 * kk:c0 + 12 * kk + 1],
                                                               in1=s3(v.h[:, :]), op0=ALU.mult, op1=ALU.add), rd=[u, PV, v], wr=[v])
            for i, j in enumerate(js):
                v = vs_[i]
                if j < 8:
                    dst, dap = XSk[j], XS.h[:, j, :]
                elif j < 10:
                    dst, dap = BT, BT.h[:, j - 8, :]
                else:
                    dst, dap = CT, CT.h[:, j - 10, :]
                op("act", lambda e: e.activation(out=dap, in_=v.h[:, :], func=AF.Silu), rd=[v], wr=[dst])
            if zj[0] < KD:
                z_block(zj[0])
                zj[0] += 1
            if j0 % 4 == 0 or j0 >= 8:
                if zj[0] < KD:
                    z_block(zj[0])
                    zj[0] += 1
        while zj[0] < KD:
            z_block(zj[0])
            zj[0] += 1
        ssd_prep(l)
        def seq_chunks(q):
            for c in range(NC_):
                yield from ssd_chunk(l, q, c, first_tile and c == 0)
        def filler():
            yield from group_a_blocks(l, first_tile)
            pt = next_proj()
            stats_mm(pt)
            op("act", lambda e: e.activation(out=RSA.h[:, :], in_=pt.h[:, 0:W], func=AF.Ln, bias=CST.h[:, 1:2], scale=1.0), rd=[pt, CST], wr=[RSA])
            op("act", lambda e: e.activation(out=RSA.h[:, :], in_=RSA.h[:, :], func=AF.Exp, scale=-0.5), rd=[RSA], wr=[RSA])
            for j in range(KD):
                op("dve", lambda e, j=j: e.tensor_tensor(out=YA.h[:, j, :], in0=YA.h[:, j, :], in1=RSA.h[:, :], op=ALU.mult), rd=[YAk[j], RSA], wr=[YAk[j]])
            yield 2.0
            for db in range(KD):
                p = next_proj()
                w = next_w(l, CH_OUT + 2 * db)
                proj_fm(p, w, YA, YAk)
                op("act", lambda e: e.activation(out=MIX.h[:, db, :], in_=p.h[:, 0:W], func=AF.Copy), rd=[p], wr=[MIXk[db]])
                yield 1.5
        interleave([seq_chunks(0), seq_chunks(1), filler()])
        op("act", lambda e: e.activation(out=RSS.h[:, :, :], in_=RSS.h[:, :, :], func=AF.Ln, bias=CST.h[:, 1:2], scale=1.0), rd=[RSS, CST], wr=[RSS])
        op("act", lambda e: e.activation(out=RSS.h[:, :, :], in_=RSS.h[:, :, :], func=AF.Exp, scale=-0.5), rd=[RSS], wr=[RSS])
        for j in range(KD):
            g = j // 4
            eng = "dve"
            op(eng, lambda e, j=j, g=g: e.tensor_tensor(out=YS.h[:, j, :], in0=YS.h[:, j, :], in1=RSS.h[:, g, :], op=ALU.mult), rd=[YSk[j], RSS], wr=[YSk[j]])
        for db in range(KD):
            p = next_proj()
            w = next_w(l, CH_OUT + 2 * db + 1)
            proj_fm(p, w, YS, YSk)
            op("dve", lambda e: e.tensor_tensor(out=MIX.h[:, db, :], in0=p.h[:, 0:W], in1=MIX.h[:, db, :], op=ALU.add), rd=[p, MIXk[db]], wr=[MIXk[db]])
            op("act", lambda e: e.activation(out=SQ8.h[:, db, :], in_=MIX.h[:, db, :], func=AF.Square), rd=[MIXk[db]], wr=[SQk[db]])
        post_norm_residual(l, PV_MIXPOST)

    NCK = 2 * NC_
    assert NCK * 16 <= 64

    def ssd_prep(l):
        pbb = l * 48
        n = NCK * 16
        PS0 = next_proj()
        for ci in range(NCK):
            sl = slice(ci * 128, (ci + 1) * 128)
            for k in range(KD):
                op("pe", lambda e, k=k: e.matmul(PS0.h[:, ci * 16:(ci + 1) * 16], lhsT=H.h[:, k, sl], rhs=WDT.h[:, l, k * 16:(k + 1) * 16],
                                                 start=(k == 0), stop=(k == KD - 1)), rd=[Hk[k], WDT], wr=[PS0], inc=(k == KD - 1))

        def c3(ap):
            return ap.rearrange("p (c h) -> p c h", h=16)

        def b3(ap16):
            return ap16.unsqueeze(1).to_broadcast([128, NCK, 16])
        op("dve", lambda e: e.tensor_tensor(out=c3(T0.h[:, 0:n]), in0=c3(PS0.h[:, 0:n]), in1=b3(PBt.h[:, pbb:pbb + 16]), op=ALU.add), rd=[PS0, PBt], wr=[T0])
        op("dve", lambda e: e.tensor_scalar(out=Mx.h[:, 0:n], in0=T0.h[:, 0:n], scalar1=0.0, scalar2=None, op0=ALU.max), rd=[T0], wr=[Mx])
        op("dve", lambda e: e.scalar_tensor_tensor(out=NA.h[:, 0:n], in0=T0.h[:, 0:n], scalar=0.0, in1=Mx.h[:, 0:n], op0=ALU.min, op1=ALU.subtract),
           rd=[T0, Mx], wr=[NA])
        op("act", lambda e: e.activation(out=NA.h[:, 0:n], in_=NA.h[:, 0:n], func=AF.Exp), rd=[NA], wr=[NA])
        op("act", lambda e: e.activation(out=NA.h[:, 0:n], in_=NA.h[:, 0:n], func=AF.Ln, bias=CST.h[:, 0:1], scale=1.0), rd=[NA, CST], wr=[NA])
        op("dve", lambda e: e.tensor_tensor(out=DT.h[:, 0:n], in0=Mx.h[:, 0:n], in1=NA.h[:, 0:n], op=ALU.add), rd=[Mx, NA], wr=[DT])
        op("dve", lambda e: e.tensor_tensor(out=c3(ADT.h[:, 0:n]), in0=c3(DT.h[:, 0:n]), in1=b3(AB.h[:, l * 16:(l + 1) * 16]), op=ALU.mult), rd=[DT, AB], wr=[ADT])
        PS1 = next_proj()
        for ci in range(NCK):
            op("pe", lambda e: e.matmul(PS1.h[:, ci * 16:(ci + 1) * 16], lhsT=UT.h[:, :], rhs=ADT.h[:, ci * 16:(ci + 1) * 16], start=True, stop=True),
               rd=[UT, ADT], wr=[PS1], inc=False)
            op("pe", lambda e: e.matmul(PS1.h[:, 64 + ci * 16:64 + (ci + 1) * 16], lhsT=ONE32.h[:, :], rhs=ADT.h[:, ci * 16:(ci + 1) * 16], start=True, stop=True),
               rd=[ONE32, ADT], wr=[PS1], inc=(ci == NCK - 1))
        op("dve", lambda e: e.tensor_scalar(out=NCS.h[:, 0:n], in0=PS1.h[:, 0:n], scalar1=-1.0, scalar2=None, op0=ALU.mult), rd=[PS1], wr=[NCS])
        op("act", lambda e: e.activation(out=ECS.h[:, 0:n], in_=PS1.h[:, 0:n], func=AF.Exp), rd=[PS1], wr=[ECS])
        op("dve", lambda e: e.tensor_tensor(out=DTE.h[:, 0:n], in0=PS1.h[:, 64:64 + n], in1=NCS.h[:, 0:n], op=ALU.add), rd=[PS1, NCS], wr=[DTE])
        op("act", lambda e: e.activation(out=DTE.h[:, 0:n], in_=DTE.h[:, 0:n], func=AF.Exp), rd=[DTE], wr=[DTE])
        op("act", lambda e: e.activation(out=CD.h[:, 0:n], in_=PS1.h[:, 64:64 + n], func=AF.Exp), rd=[PS1], wr=[CD])
        op("dve", lambda e: e.tensor_tensor(out=DTDTE.h[:, 0:n], in0=DT.h[:, 0:n], in1=DTE.h[:, 0:n], op=ALU.mult), rd=[DT, DTE], wr=[DTDTE])

    def ssd_chunk(l, q, c, first):
        ci = q * NC_ + c
        co = ci * 16
        sl = slice(ci * 128, (ci + 1) * 128)
        pbb = l * 48
        Sl = S[q][l]
        XDT, XDTE, XSD, DEC, MT, Y1, BTOK, SQC, SBF = XDTs[q], XDTEs[q], XSDs[q], DMs[q], DMs[q], Y1s[q], BTOKs[q], SQCs[q], SBFs[q]
        pbt = next_proj()
        for g in range(2):
            op("pe", lambda e, g=g: e.matmul(pbt.h[:, g * 128:(g + 1) * 128], lhsT=BT.h[:, g, sl], rhs=IBF.h[:, :], start=True, stop=True),
               rd=[BT, IBF], wr=[pbt], inc=(g == 1))
        op("act", lambda e: e.activation(out=BTOK.h[:, :, :], in_=pbt.h[:, 0:256].rearrange("p (g n) -> p g n", g=2), func=AF.Copy), rd=[pbt], wr=[BTOK])
        for g in range(2):
            px = next_proj()
            for jj in range(4):
                op("pe", lambda e, jj=jj: e.matmul(px.h[:, jj * 128:(jj + 1) * 128], lhsT=XS.h[:, 4 * g + jj, sl], rhs=IBF.h[:, :], start=True, stop=True),
                   rd=[XSk[4 * g + jj], IBF], wr=[px], inc=(jj == 3))
            op("dve", lambda e: e.tensor_tensor(out=v3(XDT[g].h[:, :]), in0=v3(px.h[:, :]), in1=bc16(DT, co + g * 8, 8), op=ALU.mult), rd=[px, DT], wr=[XDT[g]])
            op("dve", lambda e: e.tensor_tensor(out=v3(XDTE[g].h[:, :]), in0=v3(px.h[:, :]), in1=bc16(DTDTE, co + g * 8, 8), op=ALU.mult), rd=[px, DTDTE], wr=[XDTE[g]])
            op("dve", lambda e: e.tensor_tensor(out=v3(XSD[g].h[:, :]), in0=v3(px.h[:, :]),
                                                in1=PBt.h[:, pbb + 32 + g * 8:pbb + 40 + g * 8].unsqueeze(2).to_broadcast([128, 8, HP]), op=ALU.mult),
               rd=[px, PBt], wr=[XSD[g]])
        yield 4.0
        for g in range(2):
            dec, mt = DEC[g], MT[g]
            pgs = [next_proj(), next_proj()]
            for hh in range(8):
                h = co + g * 8 + hh
                pg = pgs[hh // 4]
                cs_ = slice((hh % 4) * 128, (hh % 4) * 128 + 128)
                op("pe", lambda e: e.matmul(pg.h[:, cs_], lhsT=ADT.h[:, h:h + 1].to_broadcast([128, 128]), rhs=UT.h[:, :], start=True, stop=False),
                   rd=[ADT, UT], wr=[pg], inc=False)
                if MASK_BF:
                    op("pe", lambda e: e.matmul(pg.h[:, cs_], lhsT=IBF.h[:, :], rhs=MASKB.h[:, :], start=False, stop=True),
                       rd=[IBF, MASKB], wr=[pg], inc=(hh % 4 == 3))
                else:
                    op("pe", lambda e: e.matmul(pg.h[:, cs_], lhsT=I32.h[:, :], rhs=MASK32.h[:, :], start=False, stop=True),
                       rd=[I32, MASK32], wr=[pg], inc=(hh % 4 == 3))
            for hh in range(8):
                h = co + g * 8 + hh
                pg = pgs[hh // 4]
                cs_ = slice((hh % 4) * 128, (hh % 4) * 128 + 128)
                op("act", lambda e: e.activation(out=dec.h[:, hh, :], in_=pg.h[:, cs_], func=AF.Exp, bias=NCS.h[:, h:h + 1], scale=1.0),
                   rd=[pg, NCS], wr=[dec])
            psc = next_proj()
            op("pe", lambda e: e.matmul(psc.h[:, 0:128], lhsT=BT.h[:, g, sl], rhs=CT.h[:, g, sl], start=True, stop=True), rd=[BT, CT], wr=[psc])
            op("dve", lambda e: e.tensor_tensor(out=mt.h[:, :, :], in0=dec.h[:, :, :], in1=psc.h[:, 0:128].unsqueeze(1).to_broadcast([128, 8, 128]), op=ALU.mult),
               rd=[dec, psc], wr=[mt])
            yield 3.0
        if not first:
            op("act", lambda e: e.activation(out=SBF.h[:, :], in_=Sl.h[:, :], func=AF.Copy), rd=[Sl], wr=[SBF])
        pys, pggs, pts, pgns = [], [], [], []
        for g in range(2):
            py = next_proj()
            pys.append(py)
            op("pe", lambda e: e.matmul(py.h[:, :], lhsT=IBF.h[:, :], rhs=XSD[g].h[:, :], start=True, stop=False), rd=[IBF, XSD[g]], wr=[py], inc=False)
            for hh in range(8):
                op("pe", lambda e, hh=hh: e.matmul(py.h[:, hh * HP:(hh + 1) * HP], lhsT=MT[g].h[:, hh, :], rhs=XDT[g].h[:, hh * HP:(hh + 1) * HP],
                                                   start=False, stop=(hh == 7)), rd=[MT[g], XDT[g]], wr=[py], inc=(hh == 7))
            if not first:
                pgg = next_proj()
                pggs.append(pgg)
                op("pe", lambda e: e.matmul(pgg.h[:, :], lhsT=CT.h[:, g, sl], rhs=SBF.h[:, g * 512:(g + 1) * 512], start=True, stop=True),
                   rd=[CT, SBF], wr=[pgg])
        YT = [rot(SCR), rot(SCR)]
        for g in range(2):
            if not first:
                op("dve", lambda e: e.tensor_tensor(out=v3(YT[g].h[:, :]), in0=v3(pggs[g].h[:, :]), in1=bc16(ECS, co + g * 8, 8), op=ALU.mult),
                   rd=[pggs[g], ECS], wr=[YT[g]])
                op("dve", lambda e: e.tensor_tensor(out=Y1[g].h[:, :], in0=pys[g].h[:, :], in1=YT[g].h[:, :], op=ALU.add), rd=[pys[g], YT[g]], wr=[Y1[g]])
            else:
                op("act", lambda e: e.activation(out=Y1[g].h[:, :], in_=pys[g].h[:, :], func=AF.Copy), rd=[pys[g]], wr=[Y1[g]])
        yield 4.0
        for g in range(2):
            pt = next_proj()
            pts.append(pt)
            for jj in range(4):
                op("pe", lambda e, jj=jj: e.matmul(pt.h[:, jj * 128:(jj + 1) * 128], lhsT=Y1[g].h[:, jj * 128:(jj + 1) * 128], rhs=IBF.h[:, :], start=True, stop=True),
                   rd=[Y1[g], IBF], wr=[pt], inc=(jj == 3))
        for g in range(2):
            op("dve", lambda e: e.tensor_tensor(out=YS.h[:, 4 * g:4 * g + 4, sl], in0=pts[g].h[:, :].rearrange("p (j t) -> p j t", t=128),
                                                in1=SZ.h[:, 4 * g:4 * g + 4, sl], op=ALU.mult), rd=[pts[g]] + SZk[4 * g:4 * g + 4], wr=YSk[4 * g:4 * g + 4])
            op("act", lambda e: e.activation(out=SQC[g].h[:, :, :], in_=YS.h[:, 4 * g:4 * g + 4, sl], func=AF.Square), rd=YSk[4 * g:4 * g + 4], wr=[SQC[g]])
        pgn = next_proj()
        for g in range(2):
            for kk in range(4):
                op("pe", lambda e, kk=kk: e.matmul(pgn.h[:, g * 128:(g + 1) * 128], lhsT=ONESG.h[:, :], rhs=SQC[g].h[:, kk, :], start=(kk == 0), stop=(kk == 3)),
                   rd=[ONESG, SQC[g]], wr=[pgn], inc=(kk == 3))
        op("act", lambda e: e.activation(out=RSS.h[:, :, sl], in_=pgn.h[:, 0:256].rearrange("p (g t) -> p g t", g=2), func=AF.Copy), rd=[pgn], wr=[RSS])
        yield 3.0
        for g in range(2):
            pst = next_proj()
            op("pe", lambda e: e.matmul(pst.h[:, :], lhsT=BTOK.h[:, g, :], rhs=XDTE[g].h[:, :], start=True, stop=True), rd=[BTOK, XDTE[g]], wr=[pst])
            if first:
                op("act", lambda e: e.activation(out=Sl.h[:, g * 512:(g + 1) * 512], in_=pst.h[:, :], func=AF.Copy), rd=[pst], wr=[Sl])
            else:
                op("dve", lambda e: e.tensor_tensor(out=v3(Sl.h[:, g * 512:(g + 1) * 512]), in0=v3(Sl.h[:, g * 512:(g + 1) * 512]),
                                                     in1=bc16(CD, co + g * 8, 8), op=ALU.mult), rd=[Sl, CD], wr=[Sl])
                op("dve", lambda e: e.tensor_tensor(out=Sl.h[:, g * 512:(g + 1) * 512], in0=pst.h[:, :], in1=Sl.h[:, g * 512:(g + 1) * 512], op=ALU.add),
                   rd=[pst, Sl], wr=[Sl])
        yield 2.0

    def mlp(l):
        norm_to_H(l, PV_MLPPRE, have_sq=True)
        for fb in range(32):
            w = next_w(l, CH_UP + fb)
            p = next_proj()
            proj_fm(p, w, H, Hk)
            r = rot(RR)
            op("act", lambda e: e.activation(out=r.h[:, :], in_=p.h[:, 0:W], func=AF.Relu), rd=[p], wr=[r])
            eng = "dve"
            op(eng, lambda e: e.tensor_tensor(out=FT.h[:, fb, :], in0=r.h[:, :], in1=r.h[:, :], op=ALU.mult), rd=[r], wr=[FTk[fb]])
        for db in range(KD):
            p = next_proj()
            for q in range(4):
                w = next_w(l, CH_DOWN + 4 * db + q)
                proj_fm(p, w, FT, FTk, k0=8 * q, start=(q == 0), stop=(q == 3))
            op("act", lambda e: e.activation(out=MIX.h[:, db, :], in_=p.h[:, 0:W], func=AF.Copy), rd=[p], wr=[MIXk[db]])
            op("act", lambda e: e.activation(out=SQ8.h[:, db, :], in_=p.h[:, 0:W], func=AF.Square), rd=[p], wr=[SQk[db]])
        post_norm_residual(l, PV_MLPPOST, final=(l == NL - 1))

    xv = xT_d.rearrange("(k p) n -> p k n", p=128)
    yv = yT_d.rearrange("(k p) n -> p k n", p=128)

    def whole():
        for t in range(NT):
            for q in range(2):
                tok0 = q * SEQ + t * TT
                dma(X.h[:, :, q * TT:(q + 1) * TT], xv[:, :, tok0:tok0 + TT], wr=Xk)
            for l in range(NL):
                mixer(l, t == 0, l > 0)
                mlp(l)
            for q in range(2):
                tok0 = q * SEQ + t * TT
                dma(yv[:, :, tok0:tok0 + TT], MIX.h[:, :, q * TT:(q + 1) * TT], rd=MIXk)

    P.dry = True
    whole()
    P.dry = False
    state.update({"proj": 0, "wnext": 0, "wissued": 0, "rr": 0})
    whole()
    assert state["wnext"] == len(worder)
    P.finish()
    return nc, P


def _chunk(wmat):
    return np.ascontiguousarray(wmat.reshape(KD, 128, 128).transpose(1, 0, 2)).reshape(128, 1024)


def prep_weights(w_in, w_out, w_up, w_down, NL):
    wch = np.empty((NL, NCH, 128, 1024), np.float32)
    wdt = np.empty((NL, 128, 128), np.float32)
    for l in range(NL):
        for j in range(KD):
            wch[l, 3 * j + 0] = _chunk(w_in[l][:, j * 128:(j + 1) * 128])
            wch[l, 3 * j + 1] = _chunk(w_in[l][:, 1024 + j * 128:1024 + (j + 1) * 128])
            wch[l, 3 * j + 2] = _chunk(w_in[l][:, 2048 + j * 128:2048 + (j + 1) * 128])
            wch[l, CH_Z + j] = _chunk(w_in[l][:, 3072 + j * 128:3072 + (j + 1) * 128])
        for j in range(12):
            wch[l, CH_XBC + j] = _chunk(w_in[l][:, 4096 + j * 128:4096 + (j + 1) * 128])
        for db in range(KD):
            for half in range(2):
                wch[l, CH_OUT + 2 * db + half] = _chunk(w_out[l][half * 1024:(half + 1) * 1024, db * 128:(db + 1) * 128])
            for q in range(4):
                wch[l, CH_DOWN + 4 * db + q] = _chunk(w_down[l][q * 1024:(q + 1) * 1024, db * 128:(db + 1) * 128])
        for fb in range(32):
            wch[l, CH_UP + fb] = _chunk(w_up[l][:, fb * 128:(fb + 1) * 128])
        wdt[l] = w_in[l][:, 5632:5648].reshape(KD, 128, 16).transpose(1, 0, 2).reshape(128, 128)
    return wch, wdt


def prep_params(inp, NL):
    pv = np.empty((128, NL, NPV), np.float32)
    pb = np.empty((128, NL, 48), np.float32)

    def cols(v, nb):
        v = np.asarray(v, np.float32)
        lead = v.shape[:-1]
        return np.moveaxis(v.reshape(lead + (nb, 128)), -1, 0).reshape(128, -1)

    for l in range(NL):
        pv[:, l, PV_MIXPRE:PV_MIXPRE + 8] = cols(inp["norm_mix_pre"][l], 8)
        pv[:, l, PV_CVA:PV_CVA + 24] = cols(inp["conv_a_w"][l], 8)
        pv[:, l, PV_CVS:PV_CVS + 48] = cols(inp["ssm_conv_w"][l], 12)
        pv[:, l, PV_CVB:PV_CVB + 12] = cols(inp["ssm_conv_b"][l], 12)
        pv[:, l, PV_MIXPOST:PV_MIXPOST + 8] = cols(inp["norm_mix_post"][l], 8)
        pv[:, l, PV_MLPPRE:PV_MLPPRE + 8] = cols(inp["norm_mlp_pre"][l], 8)
        pv[:, l, PV_MLPPOST:PV_MLPPOST + 8] = cols(inp["norm_mlp_post"][l], 8)
        pv[:, l, PV_GA:PV_GA + 8] = cols(inp["conv_out_norm"][l], 8)
        pv[:, l, PV_GS:PV_GS + 8] = cols(inp["ssm_out_norm"][l], 8)
        pb[:, l, 0:16] = np.asarray(inp["dt_bias"][l], np.float32)[None, :]
        pb[:, l, 16:32] = np.asarray(inp["a_log"][l], np.float32)[None, :]
        pb[:, l, 32:48] = np.asarray(inp["d_skip"][l], np.float32)[None, :]
    return pv.reshape(128, NL * NPV), pb.reshape(128, NL * 48)


def run(inp, NL=4, NB=2, SEQ=2048, TT=256, n_cores=8, trace=False):
    x = np.asarray(inp["x"], np.float32)
    wch, wdt = prep_weights(np.asarray(inp["w_in"], np.float32), np.asarray(inp["w_out"], np.float32),
                            np.asarray(inp["w_up"], np.float32), np.asarray(inp["w_down"], np.float32), NL)
    pv, pb = prep_params(inp, NL)
    nc, P = build_program(NL=NL, NB=NB, SEQ=SEQ, TT=TT)
    in_maps = []
    for c in range(n_cores):
        xs = x[c * NB:(c + 1) * NB].reshape(NB * SEQ, D)
        in_maps.append({"xT": np.ascontiguousarray(xs.T), "wch": wch, "wdt": wdt, "pv": pv, "pb": pb})
    res = run_bass_kernel_spmd(nc, in_maps, core_ids=list(range(n_cores)), trace=trace)
    out = np.empty((n_cores * NB, SEQ, D), np.float32)
    for c in range(n_cores):
        out[c * NB:(c + 1) * NB] = np.ascontiguousarray(res.results[c]["yT"].T).reshape(NB, SEQ, D)
    return out, res


def kernel(**inputs):
    out, _ = run(inputs)
    return out
```
